# Optimizing a Trainium2 kernel written in Bass

```python
import math
import jax, jax.numpy as jnp
from jax import lax
import numpy as np

D_MODEL = 2048
BATCH = 2
SEQ = 8192
DEPTH = 4

HEAD_DIM = 64
N_MIX_HEADS = D_MODEL // HEAD_DIM
A_HEADS = N_MIX_HEADS // 4
A_KV_HEADS = A_HEADS // 4
B_HEADS = N_MIX_HEADS // 4
C_HEADS = N_MIX_HEADS - A_HEADS - B_HEADS
C_KV_HEADS = C_HEADS // 4
A_W = A_HEADS * HEAD_DIM
A_KV_W = A_KV_HEADS * HEAD_DIM
B_W = B_HEADS * HEAD_DIM
C_W = C_HEADS * HEAD_DIM
C_KV_W = C_KV_HEADS * HEAD_DIM
MIX_WIDTH = A_W + B_W + C_W
IN_SIZES = [A_W, A_KV_W, A_KV_W, B_W, B_W, B_W, C_W, C_KV_W, C_KV_W]
IN_WIDTH = sum(IN_SIZES)
IN_SPLITS = [int(v) for v in np.cumsum(IN_SIZES)[:-1]]

WINDOW = 128
A_BLOCK = 128
T5_BUCKETS = 32
T5_MAX_DIST = 128
GRID_W = 64
NA_ROWS_MAX = 8
NA_COLS = 16
NA_QCOLS = 16
C_BLOCK = 128
ROPE_THETA = 10000.0
D_FF = 4 * D_MODEL
EPS = 1e-6
MASK_VALUE = -1e30

kernel_name = "hymba_style_hybrid_encoder"


def rmsnorm(x, g):
    xf = x.astype(jnp.float32)
    y = xf * lax.rsqrt(jnp.mean(xf * xf, axis=-1, keepdims=True) + EPS)
    return (y * g.astype(jnp.float32)).astype(x.dtype)


def t5_bucket(rel):
    nb = T5_BUCKETS // 2
    max_exact = nb // 2
    base = jnp.where(rel > 0, nb, 0)
    n = jnp.abs(rel)
    nf = jnp.maximum(n, 1).astype(jnp.float32)
    large = max_exact + (jnp.log(nf / max_exact) / math.log(T5_MAX_DIST / max_exact)
                         * (nb - max_exact)).astype(jnp.int32)
    large = jnp.minimum(large, nb - 1)
    return base + jnp.where(n < max_exact, n, large)


def window_attention(q, k, v, sink, t5_table):
    bsz, s_len = q.shape[0], q.shape[1]
    nb = s_len // A_BLOCK
    grp = A_HEADS // A_KV_HEADS
    qb = q.reshape(bsz, nb, A_BLOCK, A_KV_HEADS, grp, HEAD_DIM)

    def kv_blocks(t):
        tp = jnp.pad(t, ((0, 0), (A_BLOCK, A_BLOCK), (0, 0), (0, 0)))
        parts = [tp[:, o:o + s_len].reshape(bsz, nb, A_BLOCK, A_KV_HEADS, HEAD_DIM)
                 for o in (0, A_BLOCK, 2 * A_BLOCK)]
        return jnp.concatenate(parts, axis=2)

    kb, vb = kv_blocks(k), kv_blocks(v)
    qi = jnp.arange(A_BLOCK)[:, None]
    kj = jnp.arange(3 * A_BLOCK)[None, :]
    rel = kj - A_BLOCK - qi
    bias = t5_table[t5_bucket(rel)]
    bias = bias.transpose(2, 0, 1).reshape(A_KV_HEADS, grp, A_BLOCK, 3 * A_BLOCK).astype(jnp.float32)
    key_pos = jnp.arange(nb)[:, None] * A_BLOCK - A_BLOCK + jnp.arange(3 * A_BLOCK)[None, :]
    valid = ((jnp.abs(rel) <= WINDOW)[None]
             & ((key_pos >= 0) & (key_pos < s_len))[:, None, :])
    scale = HEAD_DIM ** -0.5
    s = jnp.einsum('bnqgrd,bnkgd->bngrqk', qb, kb).astype(jnp.float32) * scale + bias
    s = jnp.where(valid[None, :, None, None], s, MASK_VALUE)
    sink_l = jnp.broadcast_to(sink.astype(jnp.float32).reshape(1, 1, A_KV_HEADS, grp, 1, 1),
                              s.shape[:-1] + (1,))
    p = jax.nn.softmax(jnp.concatenate([s, sink_l], axis=-1), axis=-1)[..., :-1]
    o = jnp.einsum('bngrqk,bnkgd->bnqgrd', p.astype(v.dtype), vb)
    return o.reshape(bsz, s_len, A_W)


def neighborhood_attention(q, k, v, rpb):
    bsz, s_len = q.shape[0], q.shape[1]
    rows = s_len // GRID_W
    kr = min(NA_ROWS_MAX, rows)
    ncb = GRID_W // NA_QCOLS
    kbw = min(GRID_W, NA_COLS + NA_QCOLS)
    q5 = q.reshape(bsz, rows, GRID_W, B_HEADS, HEAD_DIM)
    k5 = k.reshape(bsz, rows, GRID_W, B_HEADS, HEAD_DIM)
    v5 = v.reshape(bsz, rows, GRID_W, B_HEADS, HEAD_DIM)
    r = jnp.arange(rows)
    row_start = jnp.clip(r - kr // 2, 0, rows - kr)
    row_idx = row_start[:, None] + jnp.arange(kr)[None, :]
    dr_idx = row_idx - r[:, None] + (NA_ROWS_MAX - 1)
    c = jnp.arange(GRID_W).reshape(ncb, NA_QCOLS)
    col_start = jnp.clip(c - NA_COLS // 2, 0, GRID_W - NA_COLS)
    kblk_start = jnp.clip(jnp.arange(ncb) * NA_QCOLS - NA_COLS // 2, 0, GRID_W - kbw)
    kcols = kblk_start[:, None] + jnp.arange(kbw)[None, :]
    kc = kcols[:, None, :]
    cs = col_start[:, :, None]
    col_valid = (kc >= cs) & (kc < cs + NA_COLS)
    dc_idx = jnp.clip(kc - c[:, :, None] + NA_COLS - 1, 0, 2 * NA_COLS - 2)
    scale = HEAD_DIM ** -0.5

    def row_block(args):
        q_r, ridx, dridx = args
        kg = k5[:, ridx][:, :, kcols]
        vg = v5[:, ridx][:, :, kcols]
        qg = q_r.reshape(bsz, ncb, NA_QCOLS, B_HEADS, HEAD_DIM)
        s = jnp.einsum('bmqhd,bimjhd->bmhqij', qg, kg).astype(jnp.float32) * scale
        bias = rpb[:, dridx][:, :, dc_idx]
        s = s + bias.transpose(2, 0, 3, 1, 4).astype(jnp.float32)[None]
        s = jnp.where(col_valid[None, :, None, :, None, :], s, MASK_VALUE)
        p = jax.nn.softmax(s.reshape(bsz, ncb, B_HEADS, NA_QCOLS, kr * kbw), axis=-1)
        p = p.reshape(s.shape).astype(v.dtype)
        o = jnp.einsum('bmhqij,bimjhd->bmqhd', p, vg)
        return o.reshape(bsz, GRID_W, B_HEADS, HEAD_DIM)

    o = lax.map(row_block, (q5.transpose(1, 0, 2, 3, 4), row_idx, dr_idx))
    return o.transpose(1, 0, 2, 3, 4).reshape(bsz, s_len, B_W)


def rope_axis(x, ang):
    x1, x2 = jnp.split(x, 2, axis=-1)
    cos = jnp.cos(ang)[None, :, None, :]
    sin = jnp.sin(ang)[None, :, None, :]
    return jnp.concatenate([x1 * cos - x2 * sin, x2 * cos + x1 * sin], axis=-1).astype(x.dtype)


def axial_rope(x, ang_row, ang_col):
    x_row, x_col = jnp.split(x, 2, axis=-1)
    return jnp.concatenate([rope_axis(x_row, ang_row), rope_axis(x_col, ang_col)], axis=-1)


def axial_global_attention(q, k, v, q_gain, k_gain):
    bsz, s_len = q.shape[0], q.shape[1]
    grp = C_HEADS // C_KV_HEADS
    t = jnp.arange(s_len)
    row = (t // GRID_W).astype(jnp.float32)
    col = (t % GRID_W).astype(jnp.float32)
    axis_dim = HEAD_DIM // 2
    freqs = ROPE_THETA ** (-jnp.arange(0, axis_dim, 2, dtype=jnp.float32) / axis_dim)
    ang_row = row[:, None] * freqs[None, :]
    ang_col = col[:, None] * freqs[None, :]
    q = axial_rope(rmsnorm(q, q_gain), ang_row, ang_col)
    k = axial_rope(rmsnorm(k, k_gain), ang_row, ang_col)
    nb = s_len // C_BLOCK
    qb = q.reshape(bsz, nb, C_BLOCK, C_KV_HEADS, grp, HEAD_DIM).transpose(1, 0, 2, 3, 4, 5)
    scale = HEAD_DIM ** -0.5

    def block(q_blk):
        s = jnp.einsum('bqgrd,bkgd->bgrqk', q_blk, k).astype(jnp.float32) * scale
        p = jax.nn.softmax(s, axis=-1).astype(v.dtype)
        return jnp.einsum('bgrqk,bkgd->bqgrd', p, v)

    o = lax.map(block, qb)
    return o.transpose(1, 0, 2, 3, 4, 5).reshape(bsz, s_len, C_W)


def setup_inputs(seed: int = 0) -> dict:
    key = jax.random.key(seed)
    ks = jax.random.split(key, 16)
    f32 = jnp.float32

    def nrm(k, shape, scale):
        return jax.random.normal(k, shape, f32) * scale

    return {
        "x": nrm(ks[0], (BATCH, SEQ, D_MODEL), 1.0),
        "norm_mix": 1.0 + nrm(ks[1], (DEPTH, D_MODEL), 0.02),
        "w_in": nrm(ks[2], (DEPTH, D_MODEL, IN_WIDTH), D_MODEL ** -0.5),
        "a_sink": nrm(ks[3], (DEPTH, A_HEADS), 0.5),
        "t5_table": nrm(ks[4], (T5_BUCKETS, A_HEADS), 0.5),
        "b_rpb": nrm(ks[5], (DEPTH, B_HEADS, 2 * NA_ROWS_MAX - 1, 2 * NA_COLS - 1), 0.5),
        "c_q_gain": 1.0 + nrm(ks[6], (DEPTH, HEAD_DIM), 0.02),
        "c_k_gain": 1.0 + nrm(ks[7], (DEPTH, HEAD_DIM), 0.02),
        "out_gain_a": 1.0 + nrm(ks[8], (DEPTH, A_W), 0.02),
        "out_gain_b": 1.0 + nrm(ks[9], (DEPTH, B_W), 0.02),
        "out_gain_c": 1.0 + nrm(ks[10], (DEPTH, C_W), 0.02),
        "w_o": nrm(ks[11], (DEPTH, MIX_WIDTH, D_MODEL), MIX_WIDTH ** -0.5),
        "norm_mlp": 1.0 + nrm(ks[12], (DEPTH, D_MODEL), 0.02),
        "w_up": nrm(ks[13], (DEPTH, D_MODEL, D_FF), D_MODEL ** -0.5),
        "w_down": nrm(ks[14], (DEPTH, D_FF, D_MODEL), D_FF ** -0.5),
        "norm_final": 1.0 + nrm(ks[15], (D_MODEL,), 0.02),
    }


def reference(x, norm_mix, w_in, a_sink, t5_table, b_rpb, c_q_gain, c_k_gain,
              out_gain_a, out_gain_b, out_gain_c, w_o, norm_mlp, w_up, w_down, norm_final):
    bsz, s_len = x.shape[0], x.shape[1]
    for l in range(DEPTH):
        h = rmsnorm(x, norm_mix[l])
        proj = jnp.einsum('bsd,de->bse', h, w_in[l])
        qa, ka, va, qb, kb, vb, qc, kc, vc = jnp.split(proj, IN_SPLITS, axis=-1)
        oa = window_attention(qa.reshape(bsz, s_len, A_HEADS, HEAD_DIM),
                              ka.reshape(bsz, s_len, A_KV_HEADS, HEAD_DIM),
                              va.reshape(bsz, s_len, A_KV_HEADS, HEAD_DIM),
                              a_sink[l], t5_table)
        ob = neighborhood_attention(qb.reshape(bsz, s_len, B_HEADS, HEAD_DIM),
                                    kb.reshape(bsz, s_len, B_HEADS, HEAD_DIM),
                                    vb.reshape(bsz, s_len, B_HEADS, HEAD_DIM),
                                    b_rpb[l])
        oc = axial_global_attention(qc.reshape(bsz, s_len, C_HEADS, HEAD_DIM),
                                    kc.reshape(bsz, s_len, C_KV_HEADS, HEAD_DIM),
                                    vc.reshape(bsz, s_len, C_KV_HEADS, HEAD_DIM),
                                    c_q_gain[l], c_k_gain[l])
        mix = jnp.concatenate([rmsnorm(oa, out_gain_a[l]),
                               rmsnorm(ob, out_gain_b[l]),
                               rmsnorm(oc, out_gain_c[l])], axis=-1)
        x = x + jnp.einsum('bse,ed->bsd', mix, w_o[l])
        h = rmsnorm(x, norm_mlp[l])
        u = jax.nn.relu(jnp.einsum('bsd,df->bsf', h, w_up[l]))
        x = x + jnp.einsum('bsf,fd->bsd', u * u, w_down[l])
    return rmsnorm(x, norm_final)
```

```python
import contextlib
import math
import numpy as np
import concourse.bass as bass
import concourse.mybir as mybir
from concourse.bass_utils import run_bass_kernel_spmd

F32 = mybir.dt.float32
BF16 = mybir.dt.bfloat16
AF = mybir.ActivationFunctionType
ALU = mybir.AluOpType

D = 2048
TOK = 2048
NT = 16
DEPTH = 4
INW = 3840
DFF = 8192
EPS = 1e-6
NEG = -30000.0
OFF = dict(qa=0, ka=512, va=640, qb=768, kb=1280, vb=1792, qc=2304, kc=3328, vc=3584)
B_RELS = [[-2, -1, 0, 1, 2, 3], [-2, -1, 0, 1, 2], [-2, -1, 0, 1, 2], [-2, -1, 0, 1, 2], [-3, -2, -1, 0, 1, 2]]
DEBUG = False
LITE = False
STOP = 9
ARENA_W = 44 * 1024


def vrow(rr, b):
    return (b // 2) * 1024 + rr * 256 + (b % 2) * 128


def btype(j):
    return 0 if j == 0 else 1 if j == 1 else 3 if j == 14 else 4 if j == 15 else 2


class _Rec:
    def __init__(self):
        self.calls = []

    def __getattr__(self, name):
        def f(*a, **k):
            self.calls.append((name, a, k))
            return None
        return f


class Sched:
    ENGS = ("pe", "act", "dve", "pool", "sp")

    def __init__(self):
        self.ops = {e: [] for e in self.ENGS}
        self.cnt = {}
        self.last_w = {}
        self.readers = {}
        self.known = {e: {} for e in self.ENGS}
        self.refd = {}
        self.amt = {}

    def op(self, eng, fn, reads=(), writes=(), dma=None, ninc=1, amount=None):
        key = dma if dma is not None else "c_" + eng
        deps = {}

        def need(d):
            for k, v in d.items():
                if k == "c_pe" and eng == "pe" and dma is None:
                    continue
                if deps.get(k, 0) < v:
                    deps[k] = v
        for r in reads:
            need(self.last_w.get(r, {}))
        for w in writes:
            need(self.last_w.get(w, {}))
            need(self.readers.get(w, {}))
        waits = []
        for k, v in deps.items():
            if self.known[eng].get(k, 0) < v:
                self.known[eng][k] = v
                waits.append((k, v))
                self.refd.setdefault(k, set()).add(v)
        rec = _Rec()
        fn(rec)
        fn = rec.calls
        idx = self.cnt.get(key, 0) + 1
        self.cnt[key] = idx
        if dma is not None:
            self.amt.setdefault(key, []).append(amount if amount is not None else 16 * ninc)
        for w in writes:
            self.last_w[w] = {key: idx}
            self.readers[w] = {}
        for r in reads:
            self.readers.setdefault(r, {})[key] = idx
        self.ops[eng].append((waits, fn, key, idx, dma is not None, amount))

    def barrier(self):
        for e in self.ENGS:
            waits = []
            for k, v in self.cnt.items():
                if self.known[e].get(k, 0) < v:
                    self.known[e][k] = v
                    waits.append((k, v))
                    self.refd.setdefault(k, set()).add(v)
            if waits:
                self.ops[e].append((waits, None, None, None, False, None))

    def emit(self, nc, stack):
        sems = {}
        for k in self.cnt:
            sems[k] = stack.enter_context(nc.semaphore("s_" + k))
        val = {}
        for k, n in self.cnt.items():
            if k in self.amt:
                acc = 0
                for i, a in enumerate(self.amt[k]):
                    acc += a
                    val[(k, i + 1)] = acc
            else:
                r = sorted(self.refd.get(k, ()))
                for rank, i in enumerate(r):
                    val[(k, i)] = rank + 1
        refd = self.refd
        block = stack.enter_context(nc.Block())
        deco = dict(pe=block.tensor, act=block.scalar, dve=block.vector, pool=block.gpsimd, sp=block.sync)

        def make(ename):
            ops = self.ops[ename]

            def body(e):
                for waits, fn, key, idx, is_dma, amount in ops:
                    for k, v in waits:
                        e.wait_ge(sems[k], val[(k, v)])
                    if fn is None:
                        continue
                    ins = [getattr(e, nm)(*a, **k) for nm, a, k in fn]
                    if is_dma:
                        for i_ in ins:
                            i_.then_inc(sems[key], amount if amount is not None else 16)
                    elif idx in refd.get(key, ()):
                        ins[-1].then_inc(sems[key], 1)
            return body
        for ename in self.ENGS:
            if self.ops[ename]:
                deco[ename](make(ename))


class Arena:
    def __init__(self, ap):
        self.ap = ap
        self.off = 0

    def reset(self, off=0):
        self.off = off

    def alloc(self, shape, dt):
        n = int(np.prod(shape[1:]))
        nw = n if dt == F32 else (n + 1) // 2
        nw = (nw + 1) // 2 * 2
        assert self.off + nw <= ARENA_W, (self.off, nw)
        v = self.ap[:, self.off:self.off + nw]
        self.off += nw
        if dt != F32:
            v = v.bitcast(dt)
        v = v[:, 0:n]
        if len(shape) == 3:
            v = v.rearrange("p (a b) -> p a b", b=shape[2])
        elif len(shape) == 4:
            v = v.rearrange("p (a b c) -> p a b c", b=shape[2], c=shape[3])
        if shape[0] < 128:
            v = v[0:shape[0]]
        return v


def build(depth, final):
    nc = bass.Bass("TRN2", target_bir_lowering=False)

    def din(name, shape, dt=F32):
        return nc.dram_tensor(name, list(shape), dt, kind="ExternalInput").ap()
    x_in = din("x", [TOK, D])
    w_in = din("w_in", [depth, D, INW])
    big = STOP >= 9
    w_o = din("w_o", [depth, D, D] if big else [1, 128, 128])
    w_up = din("w_up", [depth, D, DFF] if big else [1, 128, 128])
    w_dn = din("w_down", [depth, DFF, D] if big else [1, 128, 128])
    gvec = din("gvec", [depth * 3 + 1, 128, D])
    cols = din("cols", [128, 2 * depth + 4])
    sink = din("sink", [depth, 128, 8])
    cmat = din("cmat", [3, 128, 128])
    sel = din("sel", [128, 8])
    rope = din("rope", [2, 128, TOK])
    biasA = din("biasA", [3, 128, 3072])
    biasB = din("biasB", [depth, 5, 128, 6144])
    out = nc.dram_tensor("out", [TOK, D], F32, kind="ExternalOutput").ap()
    outn = nc.dram_tensor("outn", [TOK, D], F32, kind="ExternalOutput").ap() if final else None

    xs = nc.dram_tensor("xs", [TOK, D], F32).ap()
    qT = nc.dram_tensor("qT", [2048, TOK], BF16).ap()
    ksrc_t = nc.dram_tensor("ksrc", [896, TOK], BF16)
    kdst_t = nc.dram_tensor("kdst", [4 * 896, TOK], BF16)
    vsrc_t = nc.dram_tensor("vsrc", [TOK, 912], BF16)
    vdst_t = nc.dram_tensor("vdst", [4 * TOK, 912], BF16)
    ksrc, kdst, vsrc, vdst = ksrc_t.ap(), kdst_t.ap(), vsrc_t.ap(), vdst_t.ap()
    od = (nc.dram_tensor("od", [TOK, D], F32, kind="ExternalOutput") if DEBUG else nc.dram_tensor("od", [TOK, D], F32)).ap()

    S = Sched()
    with contextlib.ExitStack() as stack:
        arena_t = stack.enter_context(nc.sbuf_tensor("arena", [128, ARENA_W], F32))
        cm = stack.enter_context(nc.sbuf_tensor("cm", [128, 3, 128], F32))
        identb = stack.enter_context(nc.sbuf_tensor("identb", [128, 128], BF16))
        colt = stack.enter_context(nc.sbuf_tensor("colt", [128, 2 * depth + 4], F32))
        gq8 = stack.enter_context(nc.sbuf_tensor("gq8", [128, depth], F32))
        selt = stack.enter_context(nc.sbuf_tensor("selt", [128, 8], F32))
        small = stack.enter_context(nc.sbuf_tensor("small", [128, 64], F32))
        ps_t = stack.enter_context(nc.psum_tensor("ps", [128, 4096], F32))
        A = Arena(arena_t[:, :])
        ps = ps_t[:, :]
        ident_f, blk64, perm = cm[:, 0, :], cm[:, 1, :], cm[:, 2, :]
        NC0 = 2 * depth
        eps_c = colt[:, NC0:NC0 + 1]
        invn3 = colt[:, NC0 + 1:NC0 + 4]

        def bank(b, n=512, nb=1):
            return ps[:, b * 512:b * 512 + (n if nb == 1 else nb * 512)]

        def bank_bf(b, nb=1):
            return ps[:, b * 512:(b + nb) * 512].bitcast(BF16)

        S.op("sp", lambda e: [e.dma_start(out=cm[:], in_=cmat.rearrange("k p n -> p k n")),
                              e.dma_start(out=colt[:], in_=cols),
                              e.dma_start(out=selt[:], in_=sel)], writes=["const"], dma="d_const", ninc=3)
        S.op("dve", lambda e: e.tensor_copy(out=identb[:], in_=ident_f), reads=["const"], writes=["identb"])
        for l in range(depth):
            S.op("dve", lambda e, l=l: e.tensor_scalar(out=gq8[:, l:l + 1], in0=colt[:, 2 * l:2 * l + 1],
                                                       scalar1=0.125, scalar2=None, op0=ALU.mult),
                 reads=["const"], writes=["gq8"])

        def rms_to_bf16(xt, xres, gtile, gres, hb, hbres, sq, ssq, tag):
            S.op("act", lambda e: e.activation(out=sq, in_=xt, func=AF.Square), reads=[xres], writes=["sq"])
            S.op("dve", lambda e: e.tensor_reduce(out=ssq[:, 0:1], in_=sq, axis=mybir.AxisListType.X, op=ALU.add),
                 reads=["sq"], writes=["ssq"])
            S.op("act", lambda e: e.activation(out=ssq[:, 1:2], in_=ssq[:, 0:1], func=AF.Sqrt, bias=eps_c,
                                               scale=1.0 / D), reads=["ssq", "const"], writes=["ssq1"])
            S.op("dve", lambda e: e.reciprocal(out=ssq[:, 2:3], in_=ssq[:, 1:2]), reads=["ssq1"], writes=["ssq2"])
            S.op("dve", lambda e: e.scalar_tensor_tensor(out=hb, in0=xt, scalar=ssq[:, 2:3], in1=gtile,
                                                         op0=ALU.mult, op1=ALU.mult),
                 reads=[xres, "ssq2", gres], writes=[hbres])

        def transpose_to(hb, hbres, dstT, dres, t):
            pst = bank_bf(0, 2).rearrange("p (c n) -> p c n", n=128)
            for c in range(16):
                S.op("pe", lambda e, c=c: e.transpose(out=pst[:, c, :], in_=hb[:, c * 128:(c + 1) * 128],
                                                      identity=identb[:]),
                     reads=[hbres, "identb"], writes=["ps0" if c < 8 else "ps1"])
            S.op("dve", lambda e: e.tensor_copy(out=dstT[:, 0:8, t * 128:(t + 1) * 128], in_=pst[:, 0:8, :]),
                 reads=["ps0"], writes=[dres])
            S.op("act", lambda e: e.activation(out=dstT[:, 8:16, t * 128:(t + 1) * 128], in_=pst[:, 8:16, :],
                                               func=AF.Copy), reads=["ps1"], writes=[dres])

        wcnt = [0]

        def load_w(slots, src_list):
            i = wcnt[0] % len(slots)
            wcnt[0] += 1
            name = slots[i][1]
            S.op("pool", lambda e: [e.dma_start(out=d_, in_=s_) for d_, s_ in src_list(slots[i][0])],
                 writes=[name], dma="d_" + name, ninc=len(src_list(slots[i][0])))
            return slots[i]

        for l in range(depth):
            xin = x_in if l == 0 else xs
            xres = (lambda T: "xext") if l == 0 else (lambda T: ("xs", T))
            last = (l == depth - 1)
            win_l = w_in[l].rearrange("(c p) n -> p c n", p=128)
            if STOP >= 9:
                wo_l = w_o[l].rearrange("(c p) n -> p c n", p=128)
                wup_l = w_up[l].rearrange("(c p) n -> p c n", p=128)
                wdn_l = w_dn[l].rearrange("(c p) n -> p c n", p=128)

            S.barrier()
            A.reset()
            gmix = A.alloc([128, D], F32)
            xt = A.alloc([128, D], F32)
            sq = A.alloc([128, D], F32)
            hb = A.alloc([128, D], BF16)
            hT = A.alloc([128, 16, 512], BF16)
            wsl = [(A.alloc([128, 16, 512], BF16), "w%d" % i) for i in range(2)]
            ctab = A.alloc([128, TOK], F32)
            stab = A.alloc([128, TOK], F32)
            tmp = [A.alloc([128, 512], F32) for _ in range(5)]
            obf = [A.alloc([128, 512], BF16) for _ in range(2)]
            vst = A.alloc([128, 14, 65], BF16)
            ssq = small[:, 0:4]
            S.op("sp", lambda e: [e.dma_start(out=gmix, in_=gvec[3 * l]),
                                  e.dma_start(out=ctab, in_=rope[0]), e.dma_start(out=stab, in_=rope[1])],
                 writes=["gmix", "rope"], dma="d_g", ninc=3)
            S.op("pool", lambda e: e.memset(vst, 1.0), writes=["vst"])
            ocnt = 0
            for s in range(4):
                for t in range(4):
                    T = 4 * s + t
                    S.op("sp", lambda e, T=T: e.dma_start(out=xt, in_=xin[T * 128:(T + 1) * 128, :]),
                         reads=[xres(T)], writes=["xt"], dma="d_xt")
                    rms_to_bf16(xt, "xt", gmix, "gmix", hb, "hb", sq, ssq, "p1")
                    transpose_to(hb, "hb", hT, "hT", t)
                fgroups = [
                    [("qa", OFF["qa"], 512, 0, 0.125, False)],
                    [("qb", OFF["qb"], 512, 512, 0.125, False)],
                    [("kb", OFF["kb"], 512, 128, 1.0, False)],
                    [("qc", OFF["qc"], 512, 1024, None, True)],
                    [("qc", OFF["qc"] + 512, 512, 1536, None, True)],
                    [("kc", OFF["kc"], 256, 640, None, False), ("ka", OFF["ka"], 128, 0, 1.0, False)],
                ]
                pb = 0
                for grp in fgroups:
                    def srcs(slot, grp=grp):
                        r, o = [], 0
                        for (_, c0, n, _, _, _) in grp:
                            r.append((slot[:, :, o:o + n], win_l[:, :, c0:c0 + n]))
                            o += n
                        return r
                    slot, sname = load_w(wsl, srcs)
                    o = 0
                    for (kind, c0, n, drow, scale, isq) in grp:
                        for ch in range(n // 128):
                            pbank = 2 + (pb % 2)
                            pb += 1
                            pres = "ps%d" % pbank
                            for dc in range(16):
                                S.op("pe", lambda e, dc=dc, o=o, ch=ch, pbank=pbank, slot=slot:
                                     e.matmul(bank(pbank), lhsT=slot[:, dc, o + ch * 128:o + (ch + 1) * 128],
                                              rhs=hT[:, dc, :], start=(dc == 0), stop=(dc == 15)),
                                     reads=[sname, "hT"], writes=[pres])
                            ob = obf[ocnt % 2]
                            obres = "obf%d" % (ocnt % 2)
                            ocnt += 1
                            if scale is not None:
                                S.op("act", lambda e, pbank=pbank, ob=ob, scale=scale:
                                     e.activation(out=ob, in_=bank(pbank), func=AF.Copy, scale=scale),
                                     reads=[pres], writes=[obres])
                            else:
                                gcol = gq8[:, l:l + 1] if isq else colt[:, 2 * l + 1:2 * l + 2]
                                sqv, rt, rstd, qn, t1 = tmp
                                S.op("act", lambda e, pbank=pbank: e.activation(out=sqv, in_=bank(pbank), func=AF.Square),
                                     reads=[pres], writes=["t_sq"])
                                S.op("pe", lambda e: e.matmul(bank(4), lhsT=blk64, rhs=sqv, start=True, stop=True),
                                     reads=["t_sq", "const"], writes=["ps4"])
                                S.op("act", lambda e: e.activation(out=rt, in_=bank(4), func=AF.Sqrt, bias=eps_c, scale=1.0),
                                     reads=["ps4", "const"], writes=["t_rt"])
                                S.op("dve", lambda e: e.reciprocal(out=rstd, in_=rt), reads=["t_rt"], writes=["t_rstd"])
                                S.op("dve", lambda e, pbank=pbank, gcol=gcol:
                                     e.scalar_tensor_tensor(out=qn, in0=bank(pbank), scalar=gcol, in1=rstd,
                                                            op0=ALU.mult, op1=ALU.mult),
                                     reads=[pres, "t_rstd", "gq8", "const"], writes=["t_qn"])
                                S.op("pe", lambda e: e.matmul(bank(5), lhsT=perm, rhs=qn, start=True, stop=True),
                                     reads=["t_qn", "const"], writes=["ps5"])
                                S.op("dve", lambda e, s=s: e.tensor_tensor(out=t1, in0=qn, in1=ctab[:, s * 512:(s + 1) * 512],
                                                                           op=ALU.mult),
                                     reads=["t_qn", "rope"], writes=["t_t1"])
                                S.op("dve", lambda e, s=s: e.tensor_tensor(out=sqv, in0=bank(5), in1=stab[:, s * 512:(s + 1) * 512],
                                                                           op=ALU.mult),
                                     reads=["ps5", "rope"], writes=["t_sq"])
                                S.op("pool", lambda e, ob=ob: e.tensor_tensor(out=ob, in0=t1, in1=sqv, op=ALU.add),
                                     reads=["t_t1", "t_sq"], writes=[obres])
                            if kind[0] == "q":
                                dst = qT[drow + ch * 128:drow + (ch + 1) * 128, s * 512:(s + 1) * 512]
                                dres = ("qT", s)
                            else:
                                dst = ksrc[drow + ch * 128:drow + (ch + 1) * 128, s * 512:(s + 1) * 512]
                                dres = ("ksrc", drow + ch * 128, s)
                            S.op("sp", lambda e, dst=dst, ob=ob: e.dma_start(out=dst, in_=ob),
                                 reads=[obres], writes=[dres], dma="d_" + obres)
                        o += n
                s1, n1 = load_w(wsl, lambda slot: [(slot[:, :, 0:512], win_l[:, :, OFF["vb"]:OFF["vb"] + 512])])
                s2, n2 = load_w(wsl, lambda slot: [(slot[:, :, 0:128], win_l[:, :, OFF["va"]:OFF["va"] + 128]),
                                                   (slot[:, :, 128:384], win_l[:, :, OFF["vc"]:OFF["vc"] + 256])])
                for t in range(4):
                    T = 4 * s + t
                    for dc in range(16):
                        S.op("pe", lambda e, dc=dc, t=t: e.matmul(bank(6), lhsT=hT[:, dc, t * 128:(t + 1) * 128],
                                                                  rhs=s1[:, dc, 0:512], start=(dc == 0), stop=(dc == 15)),
                             reads=[n1, "hT"], writes=["ps6"])
                    for dc in range(16):
                        S.op("pe", lambda e, dc=dc, t=t: e.matmul(bank(7, 384), lhsT=hT[:, dc, t * 128:(t + 1) * 128],
                                                                  rhs=s2[:, dc, 0:384], start=(dc == 0), stop=(dc == 15)),
                             reads=[n2, "hT"], writes=["ps7"])
                    S.op("act", lambda e: e.activation(out=vst[:, 2:10, 0:64],
                                                       in_=bank(6).rearrange("p (h d) -> p h d", d=64), func=AF.Copy),
                         reads=["ps6"], writes=["vst"])
                    S.op("dve", lambda e: e.tensor_copy(out=vst[:, 0:2, 0:64],
                                                        in_=bank(7, 128).rearrange("p (h d) -> p h d", d=64)),
                         reads=["ps7"], writes=["vst"])
                    S.op("dve", lambda e: e.tensor_copy(out=vst[:, 10:14, 0:64],
                                                        in_=ps[:, 7 * 512 + 128:7 * 512 + 384].rearrange("p (h d) -> p h d", d=64)),
                         reads=["ps7"], writes=["vst"])
                    S.op("sp", lambda e, T=T: e.dma_start(out=vsrc[T * 128:(T + 1) * 128, 0:910],
                                                          in_=vst.rearrange("p h d -> p (h d)")),
                         reads=["vst"], writes=[("vsrc", T)], dma="d_vst")

            if STOP == 1:
                break
            S.barrier()
            groups = [[0, 1, 2, 3], [4, 5, 6, 7]]
            for ch in range(7):
                S.op("pool", lambda e, ch=ch: e.collective_compute(
                    "AllGather", ALU.bypass, replica_groups=groups,
                    ins=[ksrc[ch * 128:(ch + 1) * 128, :]], outs=[kdst[ch * 512:(ch + 1) * 512, :]]),
                    writes=["kdst"], dma="d_cck", amount=1)
            for ch in range(8):
                S.op("pool", lambda e, ch=ch: e.collective_compute(
                    "AllGather", ALU.bypass, replica_groups=groups,
                    ins=[vsrc[ch * 256:(ch + 1) * 256, :]], outs=[vdst[ch * 1024:(ch + 1) * 1024, :]]),
                    writes=["vdst"], dma="d_ccv", amount=1)
            S.barrier()

            if STOP == 2:
                break
            A.reset()
            kab = A.alloc([128, 5, 22 * 128], BF16)
            vab = A.alloc([128, 22, 650], BF16)
            stk = A.alloc([128, 5, 384], BF16)
            stv = A.alloc([128, 3, 650], BF16)
            expA = A.alloc([128, 3, 3072], BF16)
            expB = A.alloc([128, 6144], BF16)
            bst = A.alloc([128, 3072], F32)
            qA = A.alloc([128, 4, 128], BF16)
            qB = A.alloc([128, 4, 128], BF16)
            pT = [A.alloc([128, 1536], BF16) for _ in range(2)]
            oTs = A.alloc([128, 512], F32)
            otok = A.alloc([128, 1024], F32)
            esink = A.alloc([128, 8], F32)
            den = small[:, 8:16]
            S.op("sp", lambda e: [e.dma_start(out=kab[:, :, 384:384 + TOK],
                                              in_=ksrc[0:640, :].rearrange("(c p) t -> p c t", p=128)),
                                  e.dma_start(out=vab[:, 3:19, :],
                                              in_=vsrc[:, 0:650].rearrange("(n p) w -> p n w", p=128)),
                                  e.dma_start(out=esink, in_=sink[l])],
                 writes=["kab_own", "vab_own", "esink"], dma="d_kv", ninc=3)
            S.op("act", lambda e: e.activation(out=esink, in_=esink, func=AF.Exp), reads=["esink"], writes=["esink"])
            for side in range(2):
                tsl = slice(TOK - 384, TOK) if side == 0 else slice(0, 384)
                nsl = slice(13, 16) if side == 0 else slice(0, 3)
                kh = kab[:, :, 0:384] if side == 0 else kab[:, :, 384 + TOK:768 + TOK]
                vh = vab[:, 0:3, :] if side == 0 else vab[:, 19:22, :]
                hres = "halo%d" % side
                for rr in range(4):
                    S.op("sp", lambda e, rr=rr, tsl=tsl, nsl=nsl: [
                        e.dma_start(out=stk, in_=kdst.rearrange("(c r p) t -> r p c t", r=4, p=128)[rr][:, 0:5, tsl])] + [
                        e.dma_start(out=stv[:, bi_, :], in_=vdst[vrow(rr, b_):vrow(rr, b_) + 128, 0:650])
                        for bi_, b_ in enumerate(range(nsl.start, nsl.stop))],
                        reads=["kdst", "vdst"], writes=["stk"], dma="d_stk", ninc=4)
                    sc = selt[:, side * 4 + rr:side * 4 + rr + 1]
                    if rr == 0:
                        S.op("dve", lambda e, kh=kh, sc=sc: e.tensor_scalar(out=kh, in0=stk, scalar1=sc, scalar2=None, op0=ALU.mult),
                             reads=["stk", "const"], writes=[hres])
                        S.op("dve", lambda e, vh=vh, sc=sc: e.tensor_scalar(out=vh, in0=stv, scalar1=sc, scalar2=None, op0=ALU.mult),
                             reads=["stk", "const"], writes=[hres])
                    else:
                        S.op("dve", lambda e, kh=kh, sc=sc: e.scalar_tensor_tensor(out=kh, in0=stk, scalar=sc, in1=kh,
                                                                                   op0=ALU.mult, op1=ALU.add),
                             reads=["stk", "const", hres], writes=[hres])
                        S.op("dve", lambda e, vh=vh, sc=sc: e.scalar_tensor_tensor(out=vh, in0=stv, scalar=sc, in1=vh,
                                                                                   op0=ALU.mult, op1=ALU.add),
                             reads=["stk", "const", hres], writes=[hres])
            for ty in range(3):
                S.op("sp", lambda e, ty=ty: e.dma_start(out=bst, in_=biasA[ty]), writes=["bst"], dma="d_bst")
                S.op("act", lambda e, ty=ty: e.activation(out=expA[:, ty, :], in_=bst, func=AF.Exp),
                     reads=["bst"], writes=["expA"])
            kvres = ["kab_own", "vab_own", "halo0", "halo1"]
            cur_bt = -1
            pcnt = 0
            for j in range(NT):
                bt = btype(j)
                if bt != cur_bt:
                    cur_bt = bt
                    for hf in range(2):
                        S.op("sp", lambda e, bt=bt, hf=hf: e.dma_start(out=bst, in_=biasB[l, bt, :, hf * 3072:(hf + 1) * 3072]),
                             writes=["bst"], dma="d_bst")
                        S.op("act", lambda e, hf=hf: e.activation(out=expB[:, hf * 3072:(hf + 1) * 3072], in_=bst, func=AF.Exp),
                             reads=["bst"], writes=["expB"])
                S.op("sp", lambda e, j=j: [
                    e.dma_start(out=qA[0:64], in_=qT[0:256, j * 128:(j + 1) * 128].rearrange("(i d) t -> d i t", d=64)),
                    e.dma_start(out=qA[64:128], in_=qT[256:512, j * 128:(j + 1) * 128].rearrange("(i d) t -> d i t", d=64)),
                    e.dma_start(out=qB, in_=qT[512:1024, j * 128:(j + 1) * 128].rearrange("(c p) t -> p c t", p=128))],
                    reads=[("qT", j // 4)], writes=["qAB"], dma="d_qab", ninc=3)
                aty = 0 if j == 0 else 2 if j == NT - 1 else 1
                for g in range(2):
                    sb = 2 if pcnt % 2 == 0 else 5
                    p_ = pT[pcnt % 2]
                    pres = "pT%d" % (pcnt % 2)
                    pcnt += 1
                    sres = ["ps%d" % (sb + i) for i in range(3)]
                    for bi, rel in enumerate((-1, 0, 1)):
                        blk = 3 + j + rel
                        S.op("pe", lambda e, g=g, bi=bi, blk=blk, sb=sb:
                             e.matmul(bank(sb + bi), lhsT=kab[64 * g:64 * g + 64, 0, blk * 128:(blk + 1) * 128],
                                      rhs=qA[64 * g:64 * g + 64, :, :], start=True, stop=True),
                             reads=kvres + ["qAB"], writes=[sres[bi]])
                    S.op("act", lambda e, sb=sb, p_=p_: e.activation(out=p_, in_=bank(sb, nb=3), func=AF.Exp),
                         reads=sres, writes=[pres])
                    S.op("dve", lambda e, p_=p_, g=g, aty=aty:
                         e.tensor_tensor(out=p_, in0=p_, in1=expA[:, aty, g * 1536:(g + 1) * 1536], op=ALU.mult),
                         reads=[pres, "expA"], writes=[pres])
                    for bi, rel in enumerate((-1, 0, 1)):
                        blk = 3 + j + rel
                        S.op("pe", lambda e, g=g, bi=bi, blk=blk, p_=p_:
                             e.matmul(ps[0:65, 0:512], lhsT=vab[:, blk, g * 65:(g + 1) * 65],
                                      rhs=p_[:, bi * 512:(bi + 1) * 512], start=(bi == 0), stop=(bi == 2)),
                             reads=kvres + [pres], writes=["ps0"])
                    S.op("act", lambda e: e.activation(out=oTs[0:65, :], in_=ps[0:65, 0:512], func=AF.Copy),
                         reads=["ps0"], writes=["oTs"])
                    pst = bank(1, 264).rearrange("p (i d) -> p i d", d=66)
                    for i in range(4):
                        S.op("pe", lambda e, i=i, pst=pst: e.transpose(out=pst[:, i, 0:65], in_=oTs[0:65, i * 128:(i + 1) * 128],
                                                                       identity=ident_f[0:65, 0:65]),
                             reads=["oTs", "const"], writes=["ps1"])
                    S.op("dve", lambda e, g=g, pst=pst: e.tensor_tensor(out=den[:, 0:4], in0=pst[:, :, 64],
                                                                        in1=esink[:, 4 * g:4 * g + 4], op=ALU.add),
                         reads=["ps1", "esink"], writes=["den"])
                    S.op("dve", lambda e: e.reciprocal(out=den[:, 4:8], in_=den[:, 0:4]), reads=["den"], writes=["rden"])
                    S.op("dve", lambda e, g=g, pst=pst:
                         e.tensor_tensor(out=otok[:, g * 256:(g + 1) * 256].rearrange("p (i d) -> p i d", d=64),
                                         in0=pst[:, :, 0:64], in1=den[:, 4:8].unsqueeze(2).to_broadcast([128, 4, 64]),
                                         op=ALU.mult),
                         reads=["ps1", "rden"], writes=["otok"])
                rels = B_RELS[bt]
                for c in range(4):
                    sb = 2 if pcnt % 2 == 0 else 5
                    p_ = pT[pcnt % 2]
                    pres = "pT%d" % (pcnt % 2)
                    pcnt += 1
                    sres = ["ps%d" % (sb + i) for i in range(3)]
                    for hh in range(2):
                        for bi, rel in enumerate(rels):
                            blk = 3 + j + rel
                            o_ = sb * 512 + (hh * 6 + bi) * 128
                            S.op("pe", lambda e, hh=hh, blk=blk, o_=o_, c=c:
                                 e.matmul(ps[:, o_:o_ + 128], lhsT=kab[64 * hh:64 * hh + 64, 1 + c, blk * 128:(blk + 1) * 128],
                                          rhs=qB[64 * hh:64 * hh + 64, c, :], start=True, stop=True, skip_group_check=True),
                                 reads=kvres + ["qAB"], writes=sres)
                    S.op("act", lambda e, sb=sb, p_=p_: e.activation(out=p_, in_=bank(sb, nb=3), func=AF.Exp),
                         reads=sres, writes=[pres])
                    S.op("dve", lambda e, p_=p_, c=c:
                         e.tensor_tensor(out=p_, in0=p_, in1=expB[:, c * 1536:(c + 1) * 1536], op=ALU.mult),
                         reads=[pres, "expB"], writes=[pres])
                    for hh in range(2):
                        for bi, rel in enumerate(rels):
                            blk = 3 + j + rel
                            hv = 2 + 2 * c + hh
                            S.op("pe", lambda e, hh=hh, bi=bi, blk=blk, hv=hv, p_=p_, nb=len(rels):
                                 e.matmul(ps[0:65, hh * 128:(hh + 1) * 128], lhsT=vab[:, blk, hv * 65:(hv + 1) * 65],
                                          rhs=p_[:, (hh * 6 + bi) * 128:(hh * 6 + bi + 1) * 128],
                                          start=(bi == 0), stop=(bi == nb - 1), skip_group_check=True),
                                 reads=kvres + [pres], writes=["ps0"])
                    S.op("act", lambda e: e.activation(out=oTs[0:65, 0:256], in_=ps[0:65, 0:256], func=AF.Copy),
                         reads=["ps0"], writes=["oTs"])
                    pst = bank(1, 264).rearrange("p (i d) -> p i d", d=66)
                    for i in range(2):
                        S.op("pe", lambda e, i=i, pst=pst: e.transpose(out=pst[:, i, 0:65], in_=oTs[0:65, i * 128:(i + 1) * 128],
                                                                       identity=ident_f[0:65, 0:65]),
                             reads=["oTs", "const"], writes=["ps1"])
                    S.op("dve", lambda e, pst=pst: e.reciprocal(out=den[:, 4:6], in_=pst[:, 0:2, 64]), reads=["ps1"], writes=["rden"])
                    S.op("dve", lambda e, c=c, pst=pst:
                         e.tensor_tensor(out=otok[:, 512 + c * 128:512 + (c + 1) * 128].rearrange("p (i d) -> p i d", d=64),
                                         in0=pst[:, 0:2, 0:64], in1=den[:, 4:6].unsqueeze(2).to_broadcast([128, 2, 64]),
                                         op=ALU.mult),
                         reads=["ps1", "rden"], writes=["otok"])
                S.op("sp", lambda e, j=j: e.dma_start(out=od[j * 128:(j + 1) * 128, 0:1024], in_=otok),
                     reads=["otok"], writes=[("od", j, 0)], dma="d_otok")

            if STOP == 3:
                break
            S.barrier()
            A.reset()
            kcg = A.alloc([64, 8192], BF16)
            vcg = A.alloc([128, 64, 65], BF16)
            qcg = [A.alloc([64, 4, 256], BF16) for _ in range(2)]
            pTc = [A.alloc([128, 1024], BF16) for _ in range(2)]
            oTc = A.alloc([128, 512], F32)
            otc = [A.alloc([128, 256], F32) for _ in range(2)]
            for g in range(4):
                kch, koff = 5 + g // 2, (g % 2) * 64
                S.op("sp", lambda e, kch=kch, koff=koff, g=g: [
                    e.dma_start(out=kcg[:, rr * TOK:(rr + 1) * TOK], in_=kdst[kch * 512 + rr * 128 + koff:kch * 512 + rr * 128 + koff + 64, :])
                    for rr in range(4)] + [
                    e.dma_start(out=vcg.rearrange("p (r c h) w -> p r c h w", r=4, h=2)[:, rr, :, h_, :],
                                in_=vdst.rearrange("(c r h p) w -> r h p c w", r=4, h=2, p=128)[rr, h_][:, :, (10 + g) * 65:(11 + g) * 65])
                    for rr in range(4) for h_ in range(2)],
                    reads=["kdst", "vdst"], writes=["kvc"], dma="d_kvc", ninc=12)
                for qp in range(2 if LITE else 8):
                    qi = (g * 8 + qp) % 2
                    qc_ = qcg[qi]
                    qres = "qcg%d" % qi
                    S.op("sp", lambda e, g=g, qp=qp, qc_=qc_: e.dma_start(
                        out=qc_, in_=qT[1024 + g * 256:1024 + (g + 1) * 256, qp * 256:(qp + 1) * 256]
                        .rearrange("(i d) t -> d i t", d=64)),
                        reads=[("qT", qp // 2)], writes=[qres], dma="d_" + qres)
                    for kb in range(64):
                        st = kb % 2
                        sb = 2 + 2 * st
                        for hf in range(2):
                            S.op("pe", lambda e, kb=kb, hf=hf, sb=sb, qc_=qc_:
                                 e.matmul(bank(sb + hf), lhsT=kcg[:, kb * 128:(kb + 1) * 128],
                                          rhs=qc_[:, :, hf * 128:(hf + 1) * 128], start=True, stop=True),
                                 reads=["kvc", qres], writes=["ps%d" % (sb + hf)])
                        S.op("act", lambda e, sb=sb, st=st: e.activation(out=pTc[st], in_=bank(sb, nb=2), func=AF.Exp),
                             reads=["ps%d" % sb, "ps%d" % (sb + 1)], writes=["pTc%d" % st])
                        for hf in range(2):
                            S.op("pe", lambda e, kb=kb, hf=hf, st=st:
                                 e.matmul(ps[0:65, (6 + hf) * 512:(7 + hf) * 512], lhsT=vcg[:, kb, :],
                                          rhs=pTc[st][:, hf * 512:(hf + 1) * 512], start=(kb == 0), stop=(kb == 63)),
                                 reads=["kvc", "pTc%d" % st], writes=["ps%d" % (6 + hf)])
                    for hf in range(2):
                        T = qp * 2 + hf
                        S.op("act", lambda e, hf=hf: e.activation(out=oTc[0:65, :], in_=ps[0:65, (6 + hf) * 512:(7 + hf) * 512],
                                                                  func=AF.Copy), reads=["ps%d" % (6 + hf)], writes=["oTc"])
                        pst = bank(0, 264).rearrange("p (i d) -> p i d", d=66)
                        for i in range(4):
                            S.op("pe", lambda e, i=i, pst=pst: e.transpose(out=pst[:, i, 0:65], in_=oTc[0:65, i * 128:(i + 1) * 128],
                                                                           identity=ident_f[0:65, 0:65]),
                                 reads=["oTc", "const"], writes=["ps0"])
                        S.op("dve", lambda e, pst=pst: e.reciprocal(out=den[:, 4:8], in_=pst[:, :, 64]), reads=["ps0"], writes=["rden"])
                        S.op("dve", lambda e, hf=hf, pst=pst:
                             e.tensor_tensor(out=otc[hf].rearrange("p (i d) -> p i d", d=64), in0=pst[:, :, 0:64],
                                             in1=den[:, 4:8].unsqueeze(2).to_broadcast([128, 4, 64]), op=ALU.mult),
                             reads=["ps0", "rden"], writes=["otc%d" % hf])
                        S.op("sp", lambda e, hf=hf, T=T, g=g: e.dma_start(
                            out=od[T * 128:(T + 1) * 128, 1024 + g * 256:1024 + (g + 1) * 256], in_=otc[hf]),
                            reads=["otc%d" % hf], writes=[("od", T, 1 + g)], dma="d_otc%d" % hf)

            if STOP == 4:
                break
            S.barrier()
            A.reset()
            gout = A.alloc([128, D], F32)
            gmlp = A.alloc([128, D], F32)
            gfin = A.alloc([128, D], F32) if (last and final) else None
            xsl = A.alloc([128, 4, D], F32)
            ot = A.alloc([128, D], F32)
            sq = A.alloc([128, D], F32)
            hb = A.alloc([128, D], BF16)
            hT = A.alloc([128, 16, 512], BF16)
            mT = hT
            wsl = [(A.alloc([128, 16, 512], BF16), "w%d" % i) for i in range(4)]
            rr_ = sq
            u2 = [A.alloc([128, 4, 512], BF16) for _ in range(2)]
            ssq = small[:, 0:4]
            s3 = small[:, 16:32]
            S.op("sp", lambda e: [e.dma_start(out=gout, in_=gvec[3 * l + 2]), e.dma_start(out=gmlp, in_=gvec[3 * l + 1])]
                 + ([e.dma_start(out=gfin, in_=gvec[3 * depth])] if gfin is not None else []),
                 writes=["g3"], dma="d_g", ninc=3 if gfin is not None else 2)
            wcnt[0] = 0
            ucnt = 0
            dcnt = 0
            for s in range(1 if LITE else 4):
                S.op("sp", lambda e, s=s: e.dma_start(out=xsl, in_=xin[s * 512:(s + 1) * 512, :].rearrange("(t p) d -> p t d", p=128)),
                     reads=[xres(4 * s + t) for t in range(4)], writes=["xsl"], dma="d_xsl")
                for t in range(4):
                    T = 4 * s + t
                    S.op("sp", lambda e, T=T: e.dma_start(out=ot, in_=od[T * 128:(T + 1) * 128, :]),
                         reads=[("od", T, i) for i in range(5)], writes=["ot"], dma="d_ot")
                    for gi, (c0, c1) in enumerate(((0, 512), (512, 1024), (1024, 2048))):
                        S.op("act", lambda e, c0=c0, c1=c1: e.activation(out=sq[:, c0:c1], in_=ot[:, c0:c1], func=AF.Square),
                             reads=["ot"], writes=["sq"])
                        S.op("dve", lambda e, c0=c0, c1=c1, gi=gi: e.tensor_reduce(out=s3[:, gi:gi + 1], in_=sq[:, c0:c1],
                                                                                   axis=mybir.AxisListType.X, op=ALU.add),
                             reads=["sq"], writes=["s3"])
                    S.op("dve", lambda e: e.tensor_tensor(out=s3[:, 4:7], in0=s3[:, 0:3], in1=invn3, op=ALU.mult),
                         reads=["s3", "const"], writes=["s3b"])
                    S.op("act", lambda e: e.activation(out=s3[:, 8:11], in_=s3[:, 4:7], func=AF.Sqrt, bias=eps_c, scale=1.0),
                         reads=["s3b", "const"], writes=["s3c"])
                    S.op("dve", lambda e: e.reciprocal(out=s3[:, 12:15], in_=s3[:, 8:11]), reads=["s3c"], writes=["s3d"])
                    for gi, (c0, c1) in enumerate(((0, 512), (512, 1024), (1024, 2048))):
                        S.op("dve", lambda e, c0=c0, c1=c1, gi=gi:
                             e.scalar_tensor_tensor(out=hb[:, c0:c1], in0=ot[:, c0:c1], scalar=s3[:, 12 + gi:13 + gi],
                                                    in1=gout[:, c0:c1], op0=ALU.mult, op1=ALU.mult),
                             reads=["ot", "s3d", "g3"], writes=["hb"])
                    transpose_to(hb, "hb", mT, "hT", t)
                for ds_ in range(4):
                    slot, sname = load_w(wsl, lambda slot, ds_=ds_: [(slot, wo_l[:, :, ds_ * 512:(ds_ + 1) * 512])])
                    for t in range(4):
                        pbank = 4 + (dcnt % 4)
                        dcnt += 1
                        for ec in range(16):
                            S.op("pe", lambda e, ec=ec, t=t, pbank=pbank, slot=slot:
                                 e.matmul(bank(pbank), lhsT=mT[:, ec, t * 128:(t + 1) * 128], rhs=slot[:, ec, :],
                                          start=(ec == 0), stop=(ec == 15)),
                                 reads=[sname, "hT"], writes=["ps%d" % pbank])
                        S.op("dve", lambda e, t=t, ds_=ds_, pbank=pbank:
                             e.tensor_tensor(out=xsl[:, t, ds_ * 512:(ds_ + 1) * 512], in0=xsl[:, t, ds_ * 512:(ds_ + 1) * 512],
                                             in1=bank(pbank), op=ALU.add),
                             reads=["ps%d" % pbank, "xsl"], writes=["xsl"])
                for t in range(4):
                    rms_to_bf16(xsl[:, t, :], "xsl", gmlp, "g3", hb, "hb", sq, ssq, "p3")
                    transpose_to(hb, "hb", hT, "hT", t)
                for fg in range(16):
                    su, nu = load_w(wsl, lambda slot, fg=fg: [(slot, wup_l[:, :, fg * 512:(fg + 1) * 512])])
                    sd, nd = load_w(wsl, lambda slot, fg=fg: [(slot.rearrange("p a b -> p (a b)").rearrange("p (c n) -> p c n", n=D),
                                                               wdn_l[:, fg * 4:(fg + 1) * 4, :])])
                    sdv = sd.rearrange("p a b -> p (a b)").rearrange("p (c n) -> p c n", n=D)
                    for fc in range(4):
                        for dc in range(16):
                            S.op("pe", lambda e, fc=fc, dc=dc, su=su:
                                 e.matmul(bank(fc), lhsT=su[:, dc, fc * 128:(fc + 1) * 128], rhs=hT[:, dc, :],
                                          start=(dc == 0), stop=(dc == 15)),
                                 reads=[nu, "hT"], writes=["ps%d" % fc])
                    S.op("act", lambda e: e.activation(out=rr_, in_=bank(0, nb=4), func=AF.Relu),
                         reads=["ps0", "ps1", "ps2", "ps3"], writes=["sq"])
                    uu = u2[ucnt % 2]
                    ures = "u2%d" % (ucnt % 2)
                    ucnt += 1
                    S.op("pool", lambda e, uu=uu: e.tensor_tensor(out=uu.rearrange("p a b -> p (a b)"), in0=rr_, in1=rr_, op=ALU.mult),
                         reads=["sq"], writes=[ures])
                    for t in range(4):
                        for ds_ in range(4):
                            pbank = 4 + (dcnt % 4)
                            dcnt += 1
                            for fc in range(4):
                                S.op("pe", lambda e, fc=fc, t=t, ds_=ds_, pbank=pbank, uu=uu, sdv=sdv:
                                     e.matmul(bank(pbank), lhsT=uu[:, fc, t * 128:(t + 1) * 128],
                                              rhs=sdv[:, fc, ds_ * 512:(ds_ + 1) * 512], start=(fc == 0), stop=(fc == 3)),
                                     reads=[nd, ures], writes=["ps%d" % pbank])
                            S.op("dve", lambda e, t=t, ds_=ds_, pbank=pbank:
                                 e.tensor_tensor(out=xsl[:, t, ds_ * 512:(ds_ + 1) * 512],
                                                 in0=xsl[:, t, ds_ * 512:(ds_ + 1) * 512], in1=bank(pbank), op=ALU.add),
                                 reads=["ps%d" % pbank, "xsl"], writes=["xsl"])
                dst = out if last else xs
                S.op("sp", lambda e, s=s, dst=dst: e.dma_start(out=dst[s * 512:(s + 1) * 512, :].rearrange("(t p) d -> p t d", p=128),
                                                               in_=xsl),
                     reads=["xsl"], writes=[("xs", 4 * s + t) for t in range(4)], dma="d_xsl")
                if last and final:
                    for t in range(4):
                        T = 4 * s + t
                        S.op("act", lambda e, t=t: e.activation(out=sq, in_=xsl[:, t, :], func=AF.Square), reads=["xsl"], writes=["sq"])
                        S.op("dve", lambda e: e.tensor_reduce(out=ssq[:, 0:1], in_=sq, axis=mybir.AxisListType.X, op=ALU.add),
                             reads=["sq"], writes=["ssq"])
                        S.op("act", lambda e: e.activation(out=ssq[:, 1:2], in_=ssq[:, 0:1], func=AF.Sqrt, bias=eps_c, scale=1.0 / D),
                             reads=["ssq", "const"], writes=["ssq1"])
                        S.op("dve", lambda e: e.reciprocal(out=ssq[:, 2:3], in_=ssq[:, 1:2]), reads=["ssq1"], writes=["ssq2"])
                        S.op("dve", lambda e, t=t: e.scalar_tensor_tensor(out=ot, in0=xsl[:, t, :], scalar=ssq[:, 2:3], in1=gfin,
                                                                          op0=ALU.mult, op1=ALU.mult),
                             reads=["xsl", "ssq2", "g3"], writes=["ot"])
                        S.op("sp", lambda e, T=T: e.dma_start(out=outn[T * 128:(T + 1) * 128, :], in_=ot),
                             reads=["ot"], writes=[("outn", T)], dma="d_ot")
        S.barrier()
        S.emit(nc, stack)
    return nc


def _t5_bucket(rel):
    nb = 16
    max_exact = 8
    base = np.where(rel > 0, nb, 0)
    n = np.abs(rel)
    nf = np.maximum(n, 1).astype(np.float32)
    large = max_exact + (np.log(nf / max_exact) / math.log(128 / max_exact) * (nb - max_exact)).astype(np.int32)
    large = np.minimum(large, nb - 1)
    return base + np.where(n < max_exact, n, large)


def _bias_a(t5_table, rank):
    k = np.arange(128)[:, None]
    q = np.arange(128)[None, :]
    outp = np.full((3, 128, 2, 3, 4, 128), NEG, np.float32)
    for ty in range(3):
        for bi, relb in enumerate((-1, 0, 1)):
            rel = relb * 128 + k - q
            valid = np.abs(rel) <= 128
            if ty == 0 and rank == 0 and relb == -1:
                valid = valid & False
            if ty == 2 and rank == 3 and relb == 1:
                valid = valid & False
            bk = _t5_bucket(rel)
            for g in range(2):
                for i in range(4):
                    h = 4 * g + i
                    outp[ty, :, g, bi, i, :] = np.where(valid, t5_table[bk, h], NEG)
    return outp.reshape(3, 128, 3072)


def _bias_b(rpb, rank):
    rows = 128
    outp = np.full((5, 128, 4, 2, 6, 128), NEG, np.float32)
    k = np.arange(128)[:, None]
    q = np.arange(128)[None, :]
    for ty, jl in enumerate((0, 1, 7, 14, 15)):
        J = rank * 16 + jl
        qr = 2 * J + q // 64
        qc = q % 64
        rs = np.clip(qr - 4, 0, rows - 8)
        cs = np.clip(qc - 8, 0, 64 - 16)
        for bi, rel in enumerate(B_RELS[ty]):
            kr = 2 * (J + rel) + k // 64
            kc = k % 64
            valid = (kr >= rs) & (kr < rs + 8) & (kc >= cs) & (kc < cs + 16) & (kr >= 0) & (kr < rows)
            dr = np.clip(kr - qr + 7, 0, 14)
            dc = np.clip(kc - qc + 15, 0, 30)
            for c in range(4):
                for hh in range(2):
                    outp[ty, :, c, hh, bi, :] = np.where(valid, rpb[2 * c + hh][dr, dc], NEG)
    return outp.reshape(5, 128, 6144)


def _rope_tables(rank):
    t = np.arange(rank * TOK, (rank + 1) * TOK)
    row = (t // 64).astype(np.float32)
    col = (t % 64).astype(np.float32)
    freqs = (10000.0 ** (-np.arange(0, 32, 2, dtype=np.float32) / 32)).astype(np.float32)
    d = np.arange(128) % 64
    part = d // 32
    dd = d % 32
    f = dd % 16
    pos = np.where(part[:, None] == 0, row[None, :], col[None, :]).astype(np.float32)
    ang = (pos * freqs[f][:, None]).astype(np.float32)
    sign = np.where(dd < 16, -1.0, 1.0).astype(np.float32)[:, None]
    return np.stack([np.cos(ang), sign * np.sin(ang)]).astype(np.float32)


def _cmat():
    ident = np.eye(128, dtype=np.float32)
    blk = np.zeros((128, 128), np.float32)
    blk[:64, :64] = 1.0 / 64
    blk[64:, 64:] = 1.0 / 64
    perm = np.zeros((128, 128), np.float32)
    for m in range(128):
        dd = (m % 64) % 32
        partner = m + 16 if dd < 16 else m - 16
        perm[partner, m] = 1.0
    return np.stack([ident, blk, perm])


_NC_CACHE = {}
_DBG = {}


def _run(x_cores, layers, final, inp):
    depth = len(layers)
    key = (depth, final)
    if key not in _NC_CACHE:
        _NC_CACHE[key] = build(depth, final)
    nc = _NC_CACHE[key]
    f32 = np.float32
    ls = list(layers)
    w_in = np.ascontiguousarray(inp["w_in"][ls], f32)
    if STOP >= 9:
        w_o = np.ascontiguousarray(inp["w_o"][ls], f32)
        w_up = np.ascontiguousarray(inp["w_up"][ls], f32)
        w_dn = np.ascontiguousarray(inp["w_down"][ls], f32)
    else:
        w_o = w_up = w_dn = np.zeros((1, 128, 128), f32)
    gv = []
    for l in ls:
        gv += [inp["norm_mix"][l], inp["norm_mlp"][l],
               np.concatenate([inp["out_gain_a"][l], inp["out_gain_b"][l], inp["out_gain_c"][l]])]
    gv.append(inp["norm_final"])
    gvec = np.ascontiguousarray(np.broadcast_to(np.stack(gv).astype(f32)[:, None, :], (len(gv), 128, D)))
    cols = np.zeros((128, 2 * depth + 4), f32)
    for i, l in enumerate(ls):
        cols[:, 2 * i] = np.tile(inp["c_q_gain"][l], 2)
        cols[:, 2 * i + 1] = np.tile(inp["c_k_gain"][l], 2)
    cols[:, 2 * depth:] = np.array([EPS, 1.0 / 512, 1.0 / 512, 1.0 / 1024], f32)[None, :]
    sink = np.ascontiguousarray(np.broadcast_to(inp["a_sink"][ls].astype(f32)[:, None, :], (depth, 128, 8)))
    cm = _cmat()
    in_maps = []
    for c in range(8):
        rank = c % 4
        sel = np.zeros((128, 8), f32)
        if rank > 0:
            sel[:, rank - 1] = 1.0
        if rank < 3:
            sel[:, 4 + rank + 1] = 1.0
        in_maps.append({
            "x": x_cores[c], "w_in": w_in, "w_o": w_o, "w_up": w_up, "w_down": w_dn, "gvec": gvec, "cols": cols,
            "sink": sink, "cmat": cm, "sel": sel, "rope": _rope_tables(rank),
            "biasA": _bias_a(inp["t5_table"].astype(f32), rank),
            "biasB": np.stack([_bias_b(inp["b_rpb"][l].astype(f32), rank) for l in ls]),
        })
    res = run_bass_kernel_spmd(nc, in_maps, core_ids=list(range(8)))
    if DEBUG:
        _DBG["od"] = [np.asarray(res.results[c]["od"], dtype=f32) for c in range(8)]
    if final:
        _DBG["outn"] = [np.asarray(res.results[c]["outn"], dtype=f32) for c in range(8)]
    return [np.asarray(res.results[c]["out"], dtype=f32) for c in range(8)]


LAYERS_PER_LAUNCH = 1


def kernel(**inputs):
    inp = {k: np.asarray(v) for k, v in inputs.items()}
    x = inp["x"].astype(np.float32)
    xc = [np.ascontiguousarray(x[c // 4, (c % 4) * TOK:(c % 4 + 1) * TOK, :]) for c in range(8)]
    l0 = 0
    while l0 < DEPTH:
        ls = list(range(l0, min(DEPTH, l0 + LAYERS_PER_LAUNCH)))
        l0 = ls[-1] + 1
        xc = _run(xc, ls, True, inp)
    xc = _DBG["outn"]
    outp = np.empty((2, 8192, D), np.float32)
    for c in range(8):
        outp[c // 4, (c % 4) * TOK:(c % 4 + 1) * TOK, :] = xc[c]
    return outp
```

```python
import contextlib
import math
import numpy as np
import concourse.bass as bass
import concourse.mybir as mybir
from concourse.bass_utils import run_bass_kernel_spmd

F32 = mybir.dt.float32
BF16 = mybir.dt.bfloat16
AF = mybir.ActivationFunctionType
ALU = mybir.AluOpType

D = 2048
TOK = 2048
NT = 16
DEPTH = 4
INW = 3840
DFF = 8192
EPS = 1e-6
NEG = -30000.0
OFF = dict(qa=0, ka=512, va=640, qb=768, kb=1280, vb=1792, qc=2304, kc=3328, vc=3584)
B_RELS = [[-2, -1, 0, 1, 2, 3], [-2, -1, 0, 1, 2], [-2, -1, 0, 1, 2], [-2, -1, 0, 1, 2], [-3, -2, -1, 0, 1, 2]]
DEBUG = False
LITE = False
STOP = 9
ARENA_W = 44 * 1024


def vrow(rr, b):
    return (b // 2) * 1024 + rr * 256 + (b % 2) * 128


def btype(j):
    return 0 if j == 0 else 1 if j == 1 else 3 if j == 14 else 4 if j == 15 else 2


class _Rec:
    def __init__(self):
        self.calls = []

    def __getattr__(self, name):
        def f(*a, **k):
            self.calls.append((name, a, k))
            return None
        return f


class Sched:
    ENGS = ("pe", "act", "dve", "pool", "sp")

    def __init__(self):
        self.ops = {e: [] for e in self.ENGS}
        self.cnt = {}
        self.last_w = {}
        self.readers = {}
        self.known = {e: {} for e in self.ENGS}
        self.refd = {}
        self.amt = {}

    def op(self, eng, fn, reads=(), writes=(), dma=None, ninc=1, amount=None):
        key = dma if dma is not None else "c_" + eng
        deps = {}

        def need(d):
            for k, v in d.items():
                if k == "c_pe" and eng == "pe" and dma is None:
                    continue
                if deps.get(k, 0) < v:
                    deps[k] = v
        for r in reads:
            need(self.last_w.get(r, {}))
        for w in writes:
            need(self.last_w.get(w, {}))
            need(self.readers.get(w, {}))
        waits = []
        for k, v in deps.items():
            if self.known[eng].get(k, 0) < v:
                self.known[eng][k] = v
                waits.append((k, v))
                self.refd.setdefault(k, set()).add(v)
        rec = _Rec()
        fn(rec)
        fn = rec.calls
        idx = self.cnt.get(key, 0) + 1
        self.cnt[key] = idx
        if dma is not None:
            self.amt.setdefault(key, []).append(amount if amount is not None else 16 * ninc)
        for w in writes:
            self.last_w[w] = {key: idx}
            self.readers[w] = {}
        for r in reads:
            self.readers.setdefault(r, {})[key] = idx
        self.ops[eng].append((waits, fn, key, idx, dma is not None, amount))

    def barrier(self):
        for e in self.ENGS:
            waits = []
            for k, v in self.cnt.items():
                if self.known[e].get(k, 0) < v:
                    self.known[e][k] = v
                    waits.append((k, v))
                    self.refd.setdefault(k, set()).add(v)
            if waits:
                self.ops[e].append((waits, None, None, None, False, None))

    def emit(self, nc, stack):
        sems = {}
        for k in self.cnt:
            sems[k] = stack.enter_context(nc.semaphore("s_" + k))
        val = {}
        for k, n in self.cnt.items():
            if k in self.amt:
                acc = 0
                for i, a in enumerate(self.amt[k]):
                    acc += a
                    val[(k, i + 1)] = acc
            else:
                r = sorted(self.refd.get(k, ()))
                for rank, i in enumerate(r):
                    val[(k, i)] = rank + 1
        refd = self.refd
        block = stack.enter_context(nc.Block())
        deco = dict(pe=block.tensor, act=block.scalar, dve=block.vector, pool=block.gpsimd, sp=block.sync)

        def make(ename):
            ops = self.ops[ename]

            def body(e):
                for waits, fn, key, idx, is_dma, amount in ops:
                    for k, v in waits:
                        e.wait_ge(sems[k], val[(k, v)])
                    if fn is None:
                        continue
                    ins = [getattr(e, nm)(*a, **k) for nm, a, k in fn]
                    if is_dma:
                        for i_ in ins:
                            i_.then_inc(sems[key], amount if amount is not None else 16)
                    elif idx in refd.get(key, ()):
                        ins[-1].then_inc(sems[key], 1)
            return body
        for ename in self.ENGS:
            if self.ops[ename]:
                deco[ename](make(ename))


class Arena:
    def __init__(self, ap):
        self.ap = ap
        self.off = 0

    def reset(self, off=0):
        self.off = off

    def alloc(self, shape, dt):
        n = int(np.prod(shape[1:]))
        nw = n if dt == F32 else (n + 1) // 2
        nw = (nw + 1) // 2 * 2
        assert self.off + nw <= ARENA_W, (self.off, nw)
        v = self.ap[:, self.off:self.off + nw]
        self.off += nw
        if dt != F32:
            v = v.bitcast(dt)
        v = v[:, 0:n]
        if len(shape) == 3:
            v = v.rearrange("p (a b) -> p a b", b=shape[2])
        elif len(shape) == 4:
            v = v.rearrange("p (a b c) -> p a b c", b=shape[2], c=shape[3])
        if shape[0] < 128:
            v = v[0:shape[0]]
        return v


def build(depth, final):
    nc = bass.Bass("TRN2", target_bir_lowering=False)

    def din(name, shape, dt=F32):
        return nc.dram_tensor(name, list(shape), dt, kind="ExternalInput").ap()
    x_in = din("x", [TOK, D])
    w_in = din("w_in", [depth, D, INW])
    big = STOP >= 9
    w_o = din("w_o", [depth, D, D] if big else [1, 128, 128])
    w_up = din("w_up", [depth, D, DFF] if big else [1, 128, 128])
    w_dn = din("w_down", [depth, DFF, D] if big else [1, 128, 128])
    gvec = din("gvec", [depth * 3 + 1, 128, D])
    cols = din("cols", [128, 2 * depth + 4])
    sink = din("sink", [depth, 128, 8])
    cmat = din("cmat", [3, 128, 128])
    sel = din("sel", [128, 8])
    rope = din("rope", [2, 128, TOK])
    biasA = din("biasA", [3, 128, 3072])
    biasB = din("biasB", [depth, 5, 128, 6144])
    out = nc.dram_tensor("out", [TOK, D], F32, kind="ExternalOutput").ap()
    outn = nc.dram_tensor("outn", [TOK, D], F32, kind="ExternalOutput").ap() if final else None

    xs = nc.dram_tensor("xs", [TOK, D], F32).ap()
    qT = nc.dram_tensor("qT", [2048, TOK], BF16).ap()
    ksrc_t = nc.dram_tensor("ksrc", [896, TOK], BF16)
    kdst_t = nc.dram_tensor("kdst", [4 * 896, TOK], BF16)
    vsrc_t = nc.dram_tensor("vsrc", [TOK, 912], BF16)
    vdst_t = nc.dram_tensor("vdst", [4 * TOK, 912], BF16)
    ksrc, kdst, vsrc, vdst = ksrc_t.ap(), kdst_t.ap(), vsrc_t.ap(), vdst_t.ap()
    od = (nc.dram_tensor("od", [TOK, D], F32, kind="ExternalOutput") if DEBUG else nc.dram_tensor("od", [TOK, D], F32)).ap()

    S = Sched()
    with contextlib.ExitStack() as stack:
        arena_t = stack.enter_context(nc.sbuf_tensor("arena", [128, ARENA_W], F32))
        cm = stack.enter_context(nc.sbuf_tensor("cm", [128, 3, 128], F32))
        identb = stack.enter_context(nc.sbuf_tensor("identb", [128, 128], BF16))
        colt = stack.enter_context(nc.sbuf_tensor("colt", [128, 2 * depth + 4], F32))
        gq8 = stack.enter_context(nc.sbuf_tensor("gq8", [128, depth], F32))
        selt = stack.enter_context(nc.sbuf_tensor("selt", [128, 8], F32))
        small = stack.enter_context(nc.sbuf_tensor("small", [128, 64], F32))
        ps_t = stack.enter_context(nc.psum_tensor("ps", [128, 4096], F32))
        A = Arena(arena_t[:, :])
        ps = ps_t[:, :]
        ident_f, blk64, perm = cm[:, 0, :], cm[:, 1, :], cm[:, 2, :]
        NC0 = 2 * depth
        eps_c = colt[:, NC0:NC0 + 1]
        invn3 = colt[:, NC0 + 1:NC0 + 4]

        def bank(b, n=512, nb=1):
            return ps[:, b * 512:b * 512 + (n if nb == 1 else nb * 512)]

        def bank_bf(b, nb=1):
            return ps[:, b * 512:(b + nb) * 512].bitcast(BF16)

        S.op("sp", lambda e: [e.dma_start(out=cm[:], in_=cmat.rearrange("k p n -> p k n")),
                              e.dma_start(out=colt[:], in_=cols),
                              e.dma_start(out=selt[:], in_=sel)], writes=["const"], dma="d_const", ninc=3)
        S.op("dve", lambda e: e.tensor_copy(out=identb[:], in_=ident_f), reads=["const"], writes=["identb"])
        for l in range(depth):
            S.op("dve", lambda e, l=l: e.tensor_scalar(out=gq8[:, l:l + 1], in0=colt[:, 2 * l:2 * l + 1],
                                                       scalar1=0.125, scalar2=None, op0=ALU.mult),
                 reads=["const"], writes=["gq8"])

        def rms_to_bf16(xt, xres, gtile, gres, hb, hbres, sq, ssq, tag):
            S.op("act", lambda e: e.activation(out=sq, in_=xt, func=AF.Square), reads=[xres], writes=["sq"])
            S.op("dve", lambda e: e.tensor_reduce(out=ssq[:, 0:1], in_=sq, axis=mybir.AxisListType.X, op=ALU.add),
                 reads=["sq"], writes=["ssq"])
            S.op("act", lambda e: e.activation(out=ssq[:, 1:2], in_=ssq[:, 0:1], func=AF.Sqrt, bias=eps_c,
                                               scale=1.0 / D), reads=["ssq", "const"], writes=["ssq1"])
            S.op("dve", lambda e: e.reciprocal(out=ssq[:, 2:3], in_=ssq[:, 1:2]), reads=["ssq1"], writes=["ssq2"])
            S.op("dve", lambda e: e.scalar_tensor_tensor(out=hb, in0=xt, scalar=ssq[:, 2:3], in1=gtile,
                                                         op0=ALU.mult, op1=ALU.mult),
                 reads=[xres, "ssq2", gres], writes=[hbres])

        def transpose_to(hb, hbres, dstT, dres, t):
            pst = bank_bf(0, 2).rearrange("p (c n) -> p c n", n=128)
            for c in range(16):
                S.op("pe", lambda e, c=c: e.transpose(out=pst[:, c, :], in_=hb[:, c * 128:(c + 1) * 128],
                                                      identity=identb[:]),
                     reads=[hbres, "identb"], writes=["ps0" if c < 8 else "ps1"])
            S.op("dve", lambda e: e.tensor_copy(out=dstT[:, 0:8, t * 128:(t + 1) * 128], in_=pst[:, 0:8, :]),
                 reads=["ps0"], writes=[dres])
            S.op("act", lambda e: e.activation(out=dstT[:, 8:16, t * 128:(t + 1) * 128], in_=pst[:, 8:16, :],
                                               func=AF.Copy), reads=["ps1"], writes=[dres])

        wcnt = [0]

        def load_w(slots, src_list):
            i = wcnt[0] % len(slots)
            wcnt[0] += 1
            name = slots[i][1]
            S.op("pool", lambda e: [e.dma_start(out=d_, in_=s_) for d_, s_ in src_list(slots[i][0])],
                 writes=[name], dma="d_" + name, ninc=len(src_list(slots[i][0])))
            return slots[i]

        for l in range(depth):
            xin = x_in if l == 0 else xs
            xres = (lambda T: "xext") if l == 0 else (lambda T: ("xs", T))
            last = (l == depth - 1)
            win_l = w_in[l].rearrange("(c p) n -> p c n", p=128)
            if STOP >= 9:
                wo_l = w_o[l].rearrange("(c p) n -> p c n", p=128)
                wup_l = w_up[l].rearrange("(c p) n -> p c n", p=128)
                wdn_l = w_dn[l].rearrange("(c p) n -> p c n", p=128)

            S.barrier()
            A.reset()
            gmix = A.alloc([128, D], F32)
            xt = A.alloc([128, D], F32)
            sq = A.alloc([128, D], F32)
            hb = A.alloc([128, D], BF16)
            hT = A.alloc([128, 16, 512], BF16)
            wsl = [(A.alloc([128, 16, 512], BF16), "w%d" % i) for i in range(2)]
            ctab = A.alloc([128, TOK], F32)
            stab = A.alloc([128, TOK], F32)
            tmp = [A.alloc([128, 512], F32) for _ in range(5)]
            obf = [A.alloc([128, 512], BF16) for _ in range(2)]
            vst = A.alloc([128, 14, 65], BF16)
            ssq = small[:, 0:4]
            S.op("sp", lambda e: [e.dma_start(out=gmix, in_=gvec[3 * l]),
                                  e.dma_start(out=ctab, in_=rope[0]), e.dma_start(out=stab, in_=rope[1])],
                 writes=["gmix", "rope"], dma="d_g", ninc=3)
            S.op("pool", lambda e: e.memset(vst, 1.0), writes=["vst"])
            ocnt = 0
            for s in range(4):
                for t in range(4):
                    T = 4 * s + t
                    S.op("sp", lambda e, T=T: e.dma_start(out=xt, in_=xin[T * 128:(T + 1) * 128, :]),
                         reads=[xres(T)], writes=["xt"], dma="d_xt")
                    rms_to_bf16(xt, "xt", gmix, "gmix", hb, "hb", sq, ssq, "p1")
                    transpose_to(hb, "hb", hT, "hT", t)
                fgroups = [
                    [("qa", OFF["qa"], 512, 0, 0.125, False)],
                    [("qb", OFF["qb"], 512, 512, 0.125, False)],
                    [("kb", OFF["kb"], 512, 128, 1.0, False)],
                    [("qc", OFF["qc"], 512, 1024, None, True)],
                    [("qc", OFF["qc"] + 512, 512, 1536, None, True)],
                    [("kc", OFF["kc"], 256, 640, None, False), ("ka", OFF["ka"], 128, 0, 1.0, False)],
                ]
                pb = 0
                for grp in fgroups:
                    def srcs(slot, grp=grp):
                        r, o = [], 0
                        for (_, c0, n, _, _, _) in grp:
                            r.append((slot[:, :, o:o + n], win_l[:, :, c0:c0 + n]))
                            o += n
                        return r
                    slot, sname = load_w(wsl, srcs)
                    o = 0
                    for (kind, c0, n, drow, scale, isq) in grp:
                        for ch in range(n // 128):
                            pbank = 2 + (pb % 2)
                            pb += 1
                            pres = "ps%d" % pbank
                            for dc in range(16):
                                S.op("pe", lambda e, dc=dc, o=o, ch=ch, pbank=pbank, slot=slot:
                                     e.matmul(bank(pbank), lhsT=slot[:, dc, o + ch * 128:o + (ch + 1) * 128],
                                              rhs=hT[:, dc, :], start=(dc == 0), stop=(dc == 15)),
                                     reads=[sname, "hT"], writes=[pres])
                            ob = obf[ocnt % 2]
                            obres = "obf%d" % (ocnt % 2)
                            ocnt += 1
                            if scale is not None:
                                S.op("act", lambda e, pbank=pbank, ob=ob, scale=scale:
                                     e.activation(out=ob, in_=bank(pbank), func=AF.Copy, scale=scale),
                                     reads=[pres], writes=[obres])
                            else:
                                gcol = gq8[:, l:l + 1] if isq else colt[:, 2 * l + 1:2 * l + 2]
                                sqv, rt, rstd, qn, t1 = tmp
                                S.op("act", lambda e, pbank=pbank: e.activation(out=sqv, in_=bank(pbank), func=AF.Square),
                                     reads=[pres], writes=["t_sq"])
                                S.op("pe", lambda e: e.matmul(bank(4), lhsT=blk64, rhs=sqv, start=True, stop=True),
                                     reads=["t_sq", "const"], writes=["ps4"])
                                S.op("act", lambda e: e.activation(out=rt, in_=bank(4), func=AF.Sqrt, bias=eps_c, scale=1.0),
                                     reads=["ps4", "const"], writes=["t_rt"])
                                S.op("dve", lambda e: e.reciprocal(out=rstd, in_=rt), reads=["t_rt"], writes=["t_rstd"])
                                S.op("dve", lambda e, pbank=pbank, gcol=gcol:
                                     e.scalar_tensor_tensor(out=qn, in0=bank(pbank), scalar=gcol, in1=rstd,
                                                            op0=ALU.mult, op1=ALU.mult),
                                     reads=[pres, "t_rstd", "gq8", "const"], writes=["t_qn"])
                                S.op("pe", lambda e: e.matmul(bank(5), lhsT=perm, rhs=qn, start=True, stop=True),
                                     reads=["t_qn", "const"], writes=["ps5"])
                                S.op("dve", lambda e, s=s: e.tensor_tensor(out=t1, in0=qn, in1=ctab[:, s * 512:(s + 1) * 512],
                                                                           op=ALU.mult),
                                     reads=["t_qn", "rope"], writes=["t_t1"])
                                S.op("dve", lambda e, s=s: e.tensor_tensor(out=sqv, in0=bank(5), in1=stab[:, s * 512:(s + 1) * 512],
                                                                           op=ALU.mult),
                                     reads=["ps5", "rope"], writes=["t_sq"])
                                S.op("pool", lambda e, ob=ob: e.tensor_tensor(out=ob, in0=t1, in1=sqv, op=ALU.add),
                                     reads=["t_t1", "t_sq"], writes=[obres])
                            if kind[0] == "q":
                                dst = qT[drow + ch * 128:drow + (ch + 1) * 128, s * 512:(s + 1) * 512]
                                dres = ("qT", s)
                            else:
                                dst = ksrc[drow + ch * 128:drow + (ch + 1) * 128, s * 512:(s + 1) * 512]
                                dres = ("ksrc", drow + ch * 128, s)
                            S.op("sp", lambda e, dst=dst, ob=ob: e.dma_start(out=dst, in_=ob),
                                 reads=[obres], writes=[dres], dma="d_" + obres)
                        o += n
                s1, n1 = load_w(wsl, lambda slot: [(slot[:, :, 0:512], win_l[:, :, OFF["vb"]:OFF["vb"] + 512])])
                s2, n2 = load_w(wsl, lambda slot: [(slot[:, :, 0:128], win_l[:, :, OFF["va"]:OFF["va"] + 128]),
                                                   (slot[:, :, 128:384], win_l[:, :, OFF["vc"]:OFF["vc"] + 256])])
                for t in range(4):
                    T = 4 * s + t
                    for dc in range(16):
                        S.op("pe", lambda e, dc=dc, t=t: e.matmul(bank(6), lhsT=hT[:, dc, t * 128:(t + 1) * 128],
                                                                  rhs=s1[:, dc, 0:512], start=(dc == 0), stop=(dc == 15)),
                             reads=[n1, "hT"], writes=["ps6"])
                    for dc in range(16):
                        S.op("pe", lambda e, dc=dc, t=t: e.matmul(bank(7, 384), lhsT=hT[:, dc, t * 128:(t + 1) * 128],
                                                                  rhs=s2[:, dc, 0:384], start=(dc == 0), stop=(dc == 15)),
                             reads=[n2, "hT"], writes=["ps7"])
                    S.op("act", lambda e: e.activation(out=vst[:, 2:10, 0:64],
                                                       in_=bank(6).rearrange("p (h d) -> p h d", d=64), func=AF.Copy),
                         reads=["ps6"], writes=["vst"])
                    S.op("dve", lambda e: e.tensor_copy(out=vst[:, 0:2, 0:64],
                                                        in_=bank(7, 128).rearrange("p (h d) -> p h d", d=64)),
                         reads=["ps7"], writes=["vst"])
                    S.op("dve", lambda e: e.tensor_copy(out=vst[:, 10:14, 0:64],
                                                        in_=ps[:, 7 * 512 + 128:7 * 512 + 384].rearrange("p (h d) -> p h d", d=64)),
                         reads=["ps7"], writes=["vst"])
                    S.op("sp", lambda e, T=T: e.dma_start(out=vsrc[T * 128:(T + 1) * 128, 0:910],
                                                          in_=vst.rearrange("p h d -> p (h d)")),
                         reads=["vst"], writes=[("vsrc", T)], dma="d_vst")

            if STOP == 1:
                break
            S.barrier()
            groups = [[0, 1, 2, 3], [4, 5, 6, 7]]
            for ch in range(7):
                S.op("pool", lambda e, ch=ch: e.collective_compute(
                    "AllGather", ALU.bypass, replica_groups=groups,
                    ins=[ksrc[ch * 128:(ch + 1) * 128, :]], outs=[kdst[ch * 512:(ch + 1) * 512, :]]),
                    writes=["kdst"], dma="d_cck", amount=1)
            for ch in range(8):
                S.op("pool", lambda e, ch=ch: e.collective_compute(
                    "AllGather", ALU.bypass, replica_groups=groups,
                    ins=[vsrc[ch * 256:(ch + 1) * 256, :]], outs=[vdst[ch * 1024:(ch + 1) * 1024, :]]),
                    writes=["vdst"], dma="d_ccv", amount=1)
            S.barrier()

            if STOP == 2:
                break
            A.reset()
            kab = A.alloc([128, 5, 22 * 128], BF16)
            vab = A.alloc([128, 22, 650], BF16)
            stk = A.alloc([128, 5, 384], BF16)
            stv = A.alloc([128, 3, 650], BF16)
            expA = A.alloc([128, 3, 3072], BF16)
            expB = A.alloc([128, 6144], BF16)
            bst = A.alloc([128, 3072], F32)
            qA = A.alloc([128, 4, 128], BF16)
            qB = A.alloc([128, 4, 128], BF16)
            pT = [A.alloc([128, 1536], BF16) for _ in range(2)]
            oTs = A.alloc([128, 512], F32)
            otok = A.alloc([128, 1024], F32)
            esink = A.alloc([128, 8], F32)
            den = small[:, 8:16]
            S.op("sp", lambda e: [e.dma_start(out=kab[:, :, 384:384 + TOK],
                                              in_=ksrc[0:640, :].rearrange("(c p) t -> p c t", p=128)),
                                  e.dma_start(out=vab[:, 3:19, :],
                                              in_=vsrc[:, 0:650].rearrange("(n p) w -> p n w", p=128)),
                                  e.dma_start(out=esink, in_=sink[l])],
                 writes=["kab_own", "vab_own", "esink"], dma="d_kv", ninc=3)
            S.op("act", lambda e: e.activation(out=esink, in_=esink, func=AF.Exp), reads=["esink"], writes=["esink"])
            for side in range(2):
                tsl = slice(TOK - 384, TOK) if side == 0 else slice(0, 384)
                nsl = slice(13, 16) if side == 0 else slice(0, 3)
                kh = kab[:, :, 0:384] if side == 0 else kab[:, :, 384 + TOK:768 + TOK]
                vh = vab[:, 0:3, :] if side == 0 else vab[:, 19:22, :]
                hres = "halo%d" % side
                for rr in range(4):
                    S.op("sp", lambda e, rr=rr, tsl=tsl, nsl=nsl: [
                        e.dma_start(out=stk, in_=kdst.rearrange("(c r p) t -> r p c t", r=4, p=128)[rr][:, 0:5, tsl])] + [
                        e.dma_start(out=stv[:, bi_, :], in_=vdst[vrow(rr, b_):vrow(rr, b_) + 128, 0:650])
                        for bi_, b_ in enumerate(range(nsl.start, nsl.stop))],
                        reads=["kdst", "vdst"], writes=["stk"], dma="d_stk", ninc=4)
                    sc = selt[:, side * 4 + rr:side * 4 + rr + 1]
                    if rr == 0:
                        S.op("dve", lambda e, kh=kh, sc=sc: e.tensor_scalar(out=kh, in0=stk, scalar1=sc, scalar2=None, op0=ALU.mult),
                             reads=["stk", "const"], writes=[hres])
                        S.op("dve", lambda e, vh=vh, sc=sc: e.tensor_scalar(out=vh, in0=stv, scalar1=sc, scalar2=None, op0=ALU.mult),
                             reads=["stk", "const"], writes=[hres])
                    else:
                        S.op("dve", lambda e, kh=kh, sc=sc: e.scalar_tensor_tensor(out=kh, in0=stk, scalar=sc, in1=kh,
                                                                                   op0=ALU.mult, op1=ALU.add),
                             reads=["stk", "const", hres], writes=[hres])
                        S.op("dve", lambda e, vh=vh, sc=sc: e.scalar_tensor_tensor(out=vh, in0=stv, scalar=sc, in1=vh,
                                                                                   op0=ALU.mult, op1=ALU.add),
                             reads=["stk", "const", hres], writes=[hres])
            for ty in range(3):
                S.op("sp", lambda e, ty=ty: e.dma_start(out=bst, in_=biasA[ty]), writes=["bst"], dma="d_bst")
                S.op("act", lambda e, ty=ty: e.activation(out=expA[:, ty, :], in_=bst, func=AF.Exp),
                     reads=["bst"], writes=["expA"])
            kvres = ["kab_own", "vab_own", "halo0", "halo1"]
            cur_bt = -1
            pcnt = 0
            for j in range(NT):
                bt = btype(j)
                if bt != cur_bt:
                    cur_bt = bt
                    for hf in range(2):
                        S.op("sp", lambda e, bt=bt, hf=hf: e.dma_start(out=bst, in_=biasB[l, bt, :, hf * 3072:(hf + 1) * 3072]),
                             writes=["bst"], dma="d_bst")
                        S.op("act", lambda e, hf=hf: e.activation(out=expB[:, hf * 3072:(hf + 1) * 3072], in_=bst, func=AF.Exp),
                             reads=["bst"], writes=["expB"])
                S.op("sp", lambda e, j=j: [
                    e.dma_start(out=qA[0:64], in_=qT[0:256, j * 128:(j + 1) * 128].rearrange("(i d) t -> d i t", d=64)),
                    e.dma_start(out=qA[64:128], in_=qT[256:512, j * 128:(j + 1) * 128].rearrange("(i d) t -> d i t", d=64)),
                    e.dma_start(out=qB, in_=qT[512:1024, j * 128:(j + 1) * 128].rearrange("(c p) t -> p c t", p=128))],
                    reads=[("qT", j // 4)], writes=["qAB"], dma="d_qab", ninc=3)
                aty = 0 if j == 0 else 2 if j == NT - 1 else 1
                for g in range(2):
                    sb = 2 if pcnt % 2 == 0 else 5
                    p_ = pT[pcnt % 2]
                    pres = "pT%d" % (pcnt % 2)
                    pcnt += 1
                    sres = ["ps%d" % (sb + i) for i in range(3)]
                    for bi, rel in enumerate((-1, 0, 1)):
                        blk = 3 + j + rel
                        S.op("pe", lambda e, g=g, bi=bi, blk=blk, sb=sb:
                             e.matmul(bank(sb + bi), lhsT=kab[64 * g:64 * g + 64, 0, blk * 128:(blk + 1) * 128],
                                      rhs=qA[64 * g:64 * g + 64, :, :], start=True, stop=True),
                             reads=kvres + ["qAB"], writes=[sres[bi]])
                    S.op("act", lambda e, sb=sb, p_=p_: e.activation(out=p_, in_=bank(sb, nb=3), func=AF.Exp),
                         reads=sres, writes=[pres])
                    S.op("dve", lambda e, p_=p_, g=g, aty=aty:
                         e.tensor_tensor(out=p_, in0=p_, in1=expA[:, aty, g * 1536:(g + 1) * 1536], op=ALU.mult),
                         reads=[pres, "expA"], writes=[pres])
                    for bi, rel in enumerate((-1, 0, 1)):
                        blk = 3 + j + rel
                        S.op("pe", lambda e, g=g, bi=bi, blk=blk, p_=p_:
                             e.matmul(ps[0:65, 0:512], lhsT=vab[:, blk, g * 65:(g + 1) * 65],
                                      rhs=p_[:, bi * 512:(bi + 1) * 512], start=(bi == 0), stop=(bi == 2)),
                             reads=kvres + [pres], writes=["ps0"])
                    S.op("act", lambda e: e.activation(out=oTs[0:65, :], in_=ps[0:65, 0:512], func=AF.Copy),
                         reads=["ps0"], writes=["oTs"])
                    pst = bank(1, 264).rearrange("p (i d) -> p i d", d=66)
                    for i in range(4):
                        S.op("pe", lambda e, i=i, pst=pst: e.transpose(out=pst[:, i, 0:65], in_=oTs[0:65, i * 128:(i + 1) * 128],
                                                                       identity=ident_f[0:65, 0:65]),
                             reads=["oTs", "const"], writes=["ps1"])
                    S.op("dve", lambda e, g=g, pst=pst: e.tensor_tensor(out=den[:, 0:4], in0=pst[:, :, 64],
                                                                        in1=esink[:, 4 * g:4 * g + 4], op=ALU.add),
                         reads=["ps1", "esink"], writes=["den"])
                    S.op("dve", lambda e: e.reciprocal(out=den[:, 4:8], in_=den[:, 0:4]), reads=["den"], writes=["rden"])
                    S.op("dve", lambda e, g=g, pst=pst:
                         e.tensor_tensor(out=otok[:, g * 256:(g + 1) * 256].rearrange("p (i d) -> p i d", d=64),
                                         in0=pst[:, :, 0:64], in1=den[:, 4:8].unsqueeze(2).to_broadcast([128, 4, 64]),
                                         op=ALU.mult),
                         reads=["ps1", "rden"], writes=["otok"])
                rels = B_RELS[bt]
                for c in range(4):
                    sb = 2 if pcnt % 2 == 0 else 5
                    p_ = pT[pcnt % 2]
                    pres = "pT%d" % (pcnt % 2)
                    pcnt += 1
                    sres = ["ps%d" % (sb + i) for i in range(3)]
                    for hh in range(2):
                        for bi, rel in enumerate(rels):
                            blk = 3 + j + rel
                            o_ = sb * 512 + (hh * 6 + bi) * 128
                            S.op("pe", lambda e, hh=hh, blk=blk, o_=o_, c=c:
                                 e.matmul(ps[:, o_:o_ + 128], lhsT=kab[64 * hh:64 * hh + 64, 1 + c, blk * 128:(blk + 1) * 128],
                                          rhs=qB[64 * hh:64 * hh + 64, c, :], start=True, stop=True, skip_group_check=True),
                                 reads=kvres + ["qAB"], writes=sres)
                    S.op("act", lambda e, sb=sb, p_=p_: e.activation(out=p_, in_=bank(sb, nb=3), func=AF.Exp),
                         reads=sres, writes=[pres])
                    S.op("dve", lambda e, p_=p_, c=c:
                         e.tensor_tensor(out=p_, in0=p_, in1=expB[:, c * 1536:(c + 1) * 1536], op=ALU.mult),
                         reads=[pres, "expB"], writes=[pres])
                    for hh in range(2):
                        for bi, rel in enumerate(rels):
                            blk = 3 + j + rel
                            hv = 2 + 2 * c + hh
                            S.op("pe", lambda e, hh=hh, bi=bi, blk=blk, hv=hv, p_=p_, nb=len(rels):
                                 e.matmul(ps[0:65, hh * 128:(hh + 1) * 128], lhsT=vab[:, blk, hv * 65:(hv + 1) * 65],
                                          rhs=p_[:, (hh * 6 + bi) * 128:(hh * 6 + bi + 1) * 128],
                                          start=(bi == 0), stop=(bi == nb - 1), skip_group_check=True),
                                 reads=kvres + [pres], writes=["ps0"])
                    S.op("act", lambda e: e.activation(out=oTs[0:65, 0:256], in_=ps[0:65, 0:256], func=AF.Copy),
                         reads=["ps0"], writes=["oTs"])
                    pst = bank(1, 264).rearrange("p (i d) -> p i d", d=66)
                    for i in range(2):
                        S.op("pe", lambda e, i=i, pst=pst: e.transpose(out=pst[:, i, 0:65], in_=oTs[0:65, i * 128:(i + 1) * 128],
                                                                       identity=ident_f[0:65, 0:65]),
                             reads=["oTs", "const"], writes=["ps1"])
                    S.op("dve", lambda e, pst=pst: e.reciprocal(out=den[:, 4:6], in_=pst[:, 0:2, 64]), reads=["ps1"], writes=["rden"])
                    S.op("dve", lambda e, c=c, pst=pst:
                         e.tensor_tensor(out=otok[:, 512 + c * 128:512 + (c + 1) * 128].rearrange("p (i d) -> p i d", d=64),
                                         in0=pst[:, 0:2, 0:64], in1=den[:, 4:6].unsqueeze(2).to_broadcast([128, 2, 64]),
                                         op=ALU.mult),
                         reads=["ps1", "rden"], writes=["otok"])
                S.op("sp", lambda e, j=j: e.dma_start(out=od[j * 128:(j + 1) * 128, 0:1024], in_=otok),
                     reads=["otok"], writes=[("od", j, 0)], dma="d_otok")

            if STOP == 3:
                break
            S.barrier()
            A.reset()
            kcg = A.alloc([64, 8192], BF16)
            vcg = A.alloc([128, 64, 65], BF16)
            qcg = [A.alloc([64, 4, 256], BF16) for _ in range(2)]
            pTc = [A.alloc([128, 1024], BF16) for _ in range(2)]
            oTc = A.alloc([128, 512], F32)
            otc = [A.alloc([128, 256], F32) for _ in range(2)]
            for g in range(4):
                kch, koff = 5 + g // 2, (g % 2) * 64
                S.op("sp", lambda e, kch=kch, koff=koff, g=g: [
                    e.dma_start(out=kcg[:, rr * TOK:(rr + 1) * TOK], in_=kdst[kch * 512 + rr * 128 + koff:kch * 512 + rr * 128 + koff + 64, :])
                    for rr in range(4)] + [
                    e.dma_start(out=vcg.rearrange("p (r c h) w -> p r c h w", r=4, h=2)[:, rr, :, h_, :],
                                in_=vdst.rearrange("(c r h p) w -> r h p c w", r=4, h=2, p=128)[rr, h_][:, :, (10 + g) * 65:(11 + g) * 65])
                    for rr in range(4) for h_ in range(2)],
                    reads=["kdst", "vdst"], writes=["kvc"], dma="d_kvc", ninc=12)
                for qp in range(2 if LITE else 8):
                    qi = (g * 8 + qp) % 2
                    qc_ = qcg[qi]
                    qres = "qcg%d" % qi
                    S.op("sp", lambda e, g=g, qp=qp, qc_=qc_: e.dma_start(
                        out=qc_, in_=qT[1024 + g * 256:1024 + (g + 1) * 256, qp * 256:(qp + 1) * 256]
                        .rearrange("(i d) t -> d i t", d=64)),
                        reads=[("qT", qp // 2)], writes=[qres], dma="d_" + qres)
                    for kb in range(64):
                        st = kb % 2
                        sb = 2 + 2 * st
                        for hf in range(2):
                            S.op("pe", lambda e, kb=kb, hf=hf, sb=sb, qc_=qc_:
                                 e.matmul(bank(sb + hf), lhsT=kcg[:, kb * 128:(kb + 1) * 128],
                                          rhs=qc_[:, :, hf * 128:(hf + 1) * 128], start=True, stop=True),
                                 reads=["kvc", qres], writes=["ps%d" % (sb + hf)])
                        S.op("act", lambda e, sb=sb, st=st: e.activation(out=pTc[st], in_=bank(sb, nb=2), func=AF.Exp),
                             reads=["ps%d" % sb, "ps%d" % (sb + 1)], writes=["pTc%d" % st])
                        for hf in range(2):
                            S.op("pe", lambda e, kb=kb, hf=hf, st=st:
                                 e.matmul(ps[0:65, (6 + hf) * 512:(7 + hf) * 512], lhsT=vcg[:, kb, :],
                                          rhs=pTc[st][:, hf * 512:(hf + 1) * 512], start=(kb == 0), stop=(kb == 63)),
                                 reads=["kvc", "pTc%d" % st], writes=["ps%d" % (6 + hf)])
                    for hf in range(2):
                        T = qp * 2 + hf
                        S.op("act", lambda e, hf=hf: e.activation(out=oTc[0:65, :], in_=ps[0:65, (6 + hf) * 512:(7 + hf) * 512],
                                                                  func=AF.Copy), reads=["ps%d" % (6 + hf)], writes=["oTc"])
                        pst = bank(0, 264).rearrange("p (i d) -> p i d", d=66)
                        for i in range(4):
                            S.op("pe", lambda e, i=i, pst=pst: e.transpose(out=pst[:, i, 0:65], in_=oTc[0:65, i * 128:(i + 1) * 128],
                                                                           identity=ident_f[0:65, 0:65]),
                                 reads=["oTc", "const"], writes=["ps0"])
                        S.op("dve", lambda e, pst=pst: e.reciprocal(out=den[:, 4:8], in_=pst[:, :, 64]), reads=["ps0"], writes=["rden"])
                        S.op("dve", lambda e, hf=hf, pst=pst:
                             e.tensor_tensor(out=otc[hf].rearrange("p (i d) -> p i d", d=64), in0=pst[:, :, 0:64],
                                             in1=den[:, 4:8].unsqueeze(2).to_broadcast([128, 4, 64]), op=ALU.mult),
                             reads=["ps0", "rden"], writes=["otc%d" % hf])
                        S.op("sp", lambda e, hf=hf, T=T, g=g: e.dma_start(
                            out=od[T * 128:(T + 1) * 128, 1024 + g * 256:1024 + (g + 1) * 256], in_=otc[hf]),
                            reads=["otc%d" % hf], writes=[("od", T, 1 + g)], dma="d_otc%d" % hf)

            if STOP == 4:
                break
            S.barrier()
            A.reset()
            gout = A.alloc([128, D], F32)
            gmlp = A.alloc([128, D], F32)
            gfin = A.alloc([128, D], F32) if (last and final) else None
            xsl = A.alloc([128, 4, D], F32)
            ot = A.alloc([128, D], F32)
            sq = A.alloc([128, D], F32)
            hb = A.alloc([128, D], BF16)
            hT = A.alloc([128, 16, 512], BF16)
            mT = hT
            wsl = [(A.alloc([128, 16, 512], BF16), "w%d" % i) for i in range(4)]
            rr_ = sq
            u2 = [A.alloc([128, 4, 512], BF16) for _ in range(2)]
            ssq = small[:, 0:4]
            s3 = small[:, 16:32]
            S.op("sp", lambda e: [e.dma_start(out=gout, in_=gvec[3 * l + 2]), e.dma_start(out=gmlp, in_=gvec[3 * l + 1])]
                 + ([e.dma_start(out=gfin, in_=gvec[3 * depth])] if gfin is not None else []),
                 writes=["g3"], dma="d_g", ninc=3 if gfin is not None else 2)
            wcnt[0] = 0
            ucnt = 0
            dcnt = 0
            for s in range(1 if LITE else 4):
                S.op("sp", lambda e, s=s: e.dma_start(out=xsl, in_=xin[s * 512:(s + 1) * 512, :].rearrange("(t p) d -> p t d", p=128)),
                     reads=[xres(4 * s + t) for t in range(4)], writes=["xsl"], dma="d_xsl")
                for t in range(4):
                    T = 4 * s + t
                    S.op("sp", lambda e, T=T: e.dma_start(out=ot, in_=od[T * 128:(T + 1) * 128, :]),
                         reads=[("od", T, i) for i in range(5)], writes=["ot"], dma="d_ot")
                    for gi, (c0, c1) in enumerate(((0, 512), (512, 1024), (1024, 2048))):
                        S.op("act", lambda e, c0=c0, c1=c1: e.activation(out=sq[:, c0:c1], in_=ot[:, c0:c1], func=AF.Square),
                             reads=["ot"], writes=["sq"])
                        S.op("dve", lambda e, c0=c0, c1=c1, gi=gi: e.tensor_reduce(out=s3[:, gi:gi + 1], in_=sq[:, c0:c1],
                                                                                   axis=mybir.AxisListType.X, op=ALU.add),
                             reads=["sq"], writes=["s3"])
                    S.op("dve", lambda e: e.tensor_tensor(out=s3[:, 4:7], in0=s3[:, 0:3], in1=invn3, op=ALU.mult),
                         reads=["s3", "const"], writes=["s3b"])
                    S.op("act", lambda e: e.activation(out=s3[:, 8:11], in_=s3[:, 4:7], func=AF.Sqrt, bias=eps_c, scale=1.0),
                         reads=["s3b", "const"], writes=["s3c"])
                    S.op("dve", lambda e: e.reciprocal(out=s3[:, 12:15], in_=s3[:, 8:11]), reads=["s3c"], writes=["s3d"])
                    for gi, (c0, c1) in enumerate(((0, 512), (512, 1024), (1024, 2048))):
                        S.op("dve", lambda e, c0=c0, c1=c1, gi=gi:
                             e.scalar_tensor_tensor(out=hb[:, c0:c1], in0=ot[:, c0:c1], scalar=s3[:, 12 + gi:13 + gi],
                                                    in1=gout[:, c0:c1], op0=ALU.mult, op1=ALU.mult),
                             reads=["ot", "s3d", "g3"], writes=["hb"])
                    transpose_to(hb, "hb", mT, "hT", t)
                for ds_ in range(4):
                    slot, sname = load_w(wsl, lambda slot, ds_=ds_: [(slot, wo_l[:, :, ds_ * 512:(ds_ + 1) * 512])])
                    for t in range(4):
                        pbank = 4 + (dcnt % 4)
                        dcnt += 1
                        for ec in range(16):
                            S.op("pe", lambda e, ec=ec, t=t, pbank=pbank, slot=slot:
                                 e.matmul(bank(pbank), lhsT=mT[:, ec, t * 128:(t + 1) * 128], rhs=slot[:, ec, :],
                                          start=(ec == 0), stop=(ec == 15)),
                                 reads=[sname, "hT"], writes=["ps%d" % pbank])
                        S.op("dve", lambda e, t=t, ds_=ds_, pbank=pbank:
                             e.tensor_tensor(out=xsl[:, t, ds_ * 512:(ds_ + 1) * 512], in0=xsl[:, t, ds_ * 512:(ds_ + 1) * 512],
                                             in1=bank(pbank), op=ALU.add),
                             reads=["ps%d" % pbank, "xsl"], writes=["xsl"])
                for t in range(4):
                    rms_to_bf16(xsl[:, t, :], "xsl", gmlp, "g3", hb, "hb", sq, ssq, "p3")
                    transpose_to(hb, "hb", hT, "hT", t)
                for fg in range(16):
                    su, nu = load_w(wsl, lambda slot, fg=fg: [(slot, wup_l[:, :, fg * 512:(fg + 1) * 512])])
                    sd, nd = load_w(wsl, lambda slot, fg=fg: [(slot.rearrange("p a b -> p (a b)").rearrange("p (c n) -> p c n", n=D),
                                                               wdn_l[:, fg * 4:(fg + 1) * 4, :])])
                    sdv = sd.rearrange("p a b -> p (a b)").rearrange("p (c n) -> p c n", n=D)
                    for fc in range(4):
                        for dc in range(16):
                            S.op("pe", lambda e, fc=fc, dc=dc, su=su:
                                 e.matmul(bank(fc), lhsT=su[:, dc, fc * 128:(fc + 1) * 128], rhs=hT[:, dc, :],
                                          start=(dc == 0), stop=(dc == 15)),
                                 reads=[nu, "hT"], writes=["ps%d" % fc])
                    S.op("act", lambda e: e.activation(out=rr_, in_=bank(0, nb=4), func=AF.Relu),
                         reads=["ps0", "ps1", "ps2", "ps3"], writes=["sq"])
                    uu = u2[ucnt % 2]
                    ures = "u2%d" % (ucnt % 2)
                    ucnt += 1
                    S.op("pool", lambda e, uu=uu: e.tensor_tensor(out=uu.rearrange("p a b -> p (a b)"), in0=rr_, in1=rr_, op=ALU.mult),
                         reads=["sq"], writes=[ures])
                    for t in range(4):
                        for ds_ in range(4):
                            pbank = 4 + (dcnt % 4)
                            dcnt += 1
                            for fc in range(4):
                                S.op("pe", lambda e, fc=fc, t=t, ds_=ds_, pbank=pbank, uu=uu, sdv=sdv:
                                     e.matmul(bank(pbank), lhsT=uu[:, fc, t * 128:(t + 1) * 128],
                                              rhs=sdv[:, fc, ds_ * 512:(ds_ + 1) * 512], start=(fc == 0), stop=(fc == 3)),
                                     reads=[nd, ures], writes=["ps%d" % pbank])
                            S.op("dve", lambda e, t=t, ds_=ds_, pbank=pbank:
                                 e.tensor_tensor(out=xsl[:, t, ds_ * 512:(ds_ + 1) * 512],
                                                 in0=xsl[:, t, ds_ * 512:(ds_ + 1) * 512], in1=bank(pbank), op=ALU.add),
                                 reads=["ps%d" % pbank, "xsl"], writes=["xsl"])
                dst = out if last else xs
                S.op("sp", lambda e, s=s, dst=dst: e.dma_start(out=dst[s * 512:(s + 1) * 512, :].rearrange("(t p) d -> p t d", p=128),
                                                               in_=xsl),
                     reads=["xsl"], writes=[("xs", 4 * s + t) for t in range(4)], dma="d_xsl")
                if last and final:
                    for t in range(4):
                        T = 4 * s + t
                        S.op("act", lambda e, t=t: e.activation(out=sq, in_=xsl[:, t, :], func=AF.Square), reads=["xsl"], writes=["sq"])
                        S.op("dve", lambda e: e.tensor_reduce(out=ssq[:, 0:1], in_=sq, axis=mybir.AxisListType.X, op=ALU.add),
                             reads=["sq"], writes=["ssq"])
                        S.op("act", lambda e: e.activation(out=ssq[:, 1:2], in_=ssq[:, 0:1], func=AF.Sqrt, bias=eps_c, scale=1.0 / D),
                             reads=["ssq", "const"], writes=["ssq1"])
                        S.op("dve", lambda e: e.reciprocal(out=ssq[:, 2:3], in_=ssq[:, 1:2]), reads=["ssq1"], writes=["ssq2"])
                        S.op("dve", lambda e, t=t: e.scalar_tensor_tensor(out=ot, in0=xsl[:, t, :], scalar=ssq[:, 2:3], in1=gfin,
                                                                          op0=ALU.mult, op1=ALU.mult),
                             reads=["xsl", "ssq2", "g3"], writes=["ot"])
                        S.op("sp", lambda e, T=T: e.dma_start(out=outn[T * 128:(T + 1) * 128, :], in_=ot),
                             reads=["ot"], writes=[("outn", T)], dma="d_ot")
        S.barrier()
        S.emit(nc, stack)
    return nc


def _t5_bucket(rel):
    nb = 16
    max_exact = 8
    base = np.where(rel > 0, nb, 0)
    n = np.abs(rel)
    nf = np.maximum(n, 1).astype(np.float32)
    large = max_exact + (np.log(nf / max_exact) / math.log(128 / max_exact) * (nb - max_exact)).astype(np.int32)
    large = np.minimum(large, nb - 1)
    return base + np.where(n < max_exact, n, large)


def _bias_a(t5_table, rank):
    k = np.arange(128)[:, None]
    q = np.arange(128)[None, :]
    outp = np.full((3, 128, 2, 3, 4, 128), NEG, np.float32)
    for ty in range(3):
        for bi, relb in enumerate((-1, 0, 1)):
            rel = relb * 128 + k - q
            valid = np.abs(rel) <= 128
            if ty == 0 and rank == 0 and relb == -1:
                valid = valid & False
            if ty == 2 and rank == 3 and relb == 1:
                valid = valid & False
            bk = _t5_bucket(rel)
            for g in range(2):
                for i in range(4):
                    h = 4 * g + i
                    outp[ty, :, g, bi, i, :] = np.where(valid, t5_table[bk, h], NEG)
    return outp.reshape(3, 128, 3072)


def _bias_b(rpb, rank):
    rows = 128
    outp = np.full((5, 128, 4, 2, 6, 128), NEG, np.float32)
    k = np.arange(128)[:, None]
    q = np.arange(128)[None, :]
    for ty, jl in enumerate((0, 1, 7, 14, 15)):
        J = rank * 16 + jl
        qr = 2 * J + q // 64
        qc = q % 64
        rs = np.clip(qr - 4, 0, rows - 8)
        cs = np.clip(qc - 8, 0, 64 - 16)
        for bi, rel in enumerate(B_RELS[ty]):
            kr = 2 * (J + rel) + k // 64
            kc = k % 64
            valid = (kr >= rs) & (kr < rs + 8) & (kc >= cs) & (kc < cs + 16) & (kr >= 0) & (kr < rows)
            dr = np.clip(kr - qr + 7, 0, 14)
            dc = np.clip(kc - qc + 15, 0, 30)
            for c in range(4):
                for hh in range(2):
                    outp[ty, :, c, hh, bi, :] = np.where(valid, rpb[2 * c + hh][dr, dc], NEG)
    return outp.reshape(5, 128, 6144)


def _rope_tables(rank):
    t = np.arange(rank * TOK, (rank + 1) * TOK)
    row = (t // 64).astype(np.float32)
    col = (t % 64).astype(np.float32)
    freqs = (10000.0 ** (-np.arange(0, 32, 2, dtype=np.float32) / 32)).astype(np.float32)
    d = np.arange(128) % 64
    part = d // 32
    dd = d % 32
    f = dd % 16
    pos = np.where(part[:, None] == 0, row[None, :], col[None, :]).astype(np.float32)
    ang = (pos * freqs[f][:, None]).astype(np.float32)
    sign = np.where(dd < 16, -1.0, 1.0).astype(np.float32)[:, None]
    return np.stack([np.cos(ang), sign * np.sin(ang)]).astype(np.float32)


def _cmat():
    ident = np.eye(128, dtype=np.float32)
    blk = np.zeros((128, 128), np.float32)
    blk[:64, :64] = 1.0 / 64
    blk[64:, 64:] = 1.0 / 64
    perm = np.zeros((128, 128), np.float32)
    for m in range(128):
        dd = (m % 64) % 32
        partner = m + 16 if dd < 16 else m - 16
        perm[partner, m] = 1.0
    return np.stack([ident, blk, perm])


_NC_CACHE = {}
_DBG = {}


def _run(x_cores, layers, final, inp):
    depth = len(layers)
    key = (depth, final)
    if key not in _NC_CACHE:
        _NC_CACHE[key] = build(depth, final)
    nc = _NC_CACHE[key]
    f32 = np.float32
    ls = list(layers)
    w_in = np.ascontiguousarray(inp["w_in"][ls], f32)
    if STOP >= 9:
        w_o = np.ascontiguousarray(inp["w_o"][ls], f32)
        w_up = np.ascontiguousarray(inp["w_up"][ls], f32)
        w_dn = np.ascontiguousarray(inp["w_down"][ls], f32)
    else:
        w_o = w_up = w_dn = np.zeros((1, 128, 128), f32)
    gv = []
    for l in ls:
        gv += [inp["norm_mix"][l], inp["norm_mlp"][l],
               np.concatenate([inp["out_gain_a"][l], inp["out_gain_b"][l], inp["out_gain_c"][l]])]
    gv.append(inp["norm_final"])
    gvec = np.ascontiguousarray(np.broadcast_to(np.stack(gv).astype(f32)[:, None, :], (len(gv), 128, D)))
    cols = np.zeros((128, 2 * depth + 4), f32)
    for i, l in enumerate(ls):
        cols[:, 2 * i] = np.tile(inp["c_q_gain"][l], 2)
        cols[:, 2 * i + 1] = np.tile(inp["c_k_gain"][l], 2)
    cols[:, 2 * depth:] = np.array([EPS, 1.0 / 512, 1.0 / 512, 1.0 / 1024], f32)[None, :]
    sink = np.ascontiguousarray(np.broadcast_to(inp["a_sink"][ls].astype(f32)[:, None, :], (depth, 128, 8)))
    cm = _cmat()
    in_maps = []
    for c in range(8):
        rank = c % 4
        sel = np.zeros((128, 8), f32)
        if rank > 0:
            sel[:, rank - 1] = 1.0
        if rank < 3:
            sel[:, 4 + rank + 1] = 1.0
        in_maps.append({
            "x": x_cores[c], "w_in": w_in, "w_o": w_o, "w_up": w_up, "w_down": w_dn, "gvec": gvec, "cols": cols,
            "sink": sink, "cmat": cm, "sel": sel, "rope": _rope_tables(rank),
            "biasA": _bias_a(inp["t5_table"].astype(f32), rank),
            "biasB": np.stack([_bias_b(inp["b_rpb"][l].astype(f32), rank) for l in ls]),
        })
    res = run_bass_kernel_spmd(nc, in_maps, core_ids=list(range(8)))
    if DEBUG:
        _DBG["od"] = [np.asarray(res.results[c]["od"], dtype=f32) for c in range(8)]
    if final:
        _DBG["outn"] = [np.asarray(res.results[c]["outn"], dtype=f32) for c in range(8)]
    return [np.asarray(res.results[c]["out"], dtype=f32) for c in range(8)]


LAYERS_PER_LAUNCH = 4


def kernel(**inputs):
    inp = {k: np.asarray(v) for k, v in inputs.items()}
    x = inp["x"].astype(np.float32)
    xc = [np.ascontiguousarray(x[c // 4, (c % 4) * TOK:(c % 4 + 1) * TOK, :]) for c in range(8)]
    l0 = 0
    while l0 < DEPTH:
        ls = list(range(l0, min(DEPTH, l0 + LAYERS_PER_LAUNCH)))
        l0 = ls[-1] + 1
        xc = _run(xc, ls, True, inp)
    xc = _DBG["outn"]
    outp = np.empty((2, 8192, D), np.float32)
    for c in range(8):
        outp[c // 4, (c % 4) * TOK:(c % 4 + 1) * TOK, :] = xc[c]
    return outp
```
